# Optimizing a Trainium2 kernel written in Bass

```python
import math
import jax, jax.numpy as jnp
from jax import lax
import numpy as np

D_MODEL = 1024
BATCH = 2
SEQ = 8192
DEPTH = 2

MEM_TOKENS = 256
ROPE_THETA = 10000.0
RMS_EPS = 1e-6
Q_BLOCK = 128
NEG_INF = -1e30

MLA_HEADS = 8
Q_LORA = 256
KV_LORA = 128
MLA_NOPE = 64
MLA_ROPE = 32
MLA_QK = MLA_NOPE + MLA_ROPE
MLA_V = 64
MLA_W = MLA_HEADS * MLA_V

DIFF_HEADS = 4
DIFF_D = 32
DIFF_V = 2 * DIFF_D
DIFF_W = DIFF_HEADS * DIFF_V

MEM_HEADS = 4
MEM_D = 64
MEM_W = MEM_HEADS * MEM_D

D_MIX = MLA_W + DIFF_W + MEM_W

IN_SPLITS = [
    Q_LORA,
    KV_LORA,
    MLA_ROPE,
    DIFF_HEADS * 2 * DIFF_D,
    DIFF_HEADS * 2 * DIFF_D,
    DIFF_W,
    MEM_W,
    D_MIX,
]
D_IN = int(sum(IN_SPLITS))
IN_OFFSETS = [int(v) for v in np.cumsum(IN_SPLITS)[:-1]]

kernel_name = "hymba_mla_diffattn_memory_block"


def rms_norm(x, g, eps=RMS_EPS):
    xf = x.astype(jnp.float32)
    y = xf * lax.rsqrt(jnp.mean(xf * xf, axis=-1, keepdims=True) + eps)
    return (y * g.astype(jnp.float32)).astype(x.dtype)


def rope(x, pos):
    d = x.shape[-1]
    inv = ROPE_THETA ** (-jnp.arange(0, d, 2, dtype=jnp.float32) / d)
    ang = pos.astype(jnp.float32)[..., None] * inv
    cos = jnp.cos(ang)[:, :, None, :]
    sin = jnp.sin(ang)[:, :, None, :]
    xf = x.astype(jnp.float32)
    x1, x2 = xf[..., : d // 2], xf[..., d // 2:]
    out = jnp.concatenate([x1 * cos - x2 * sin, x2 * cos + x1 * sin], axis=-1)
    return out.astype(x.dtype)


def causal_mask(start, s_len):
    q_idx = start + jnp.arange(Q_BLOCK, dtype=jnp.int32)
    return jnp.arange(s_len, dtype=jnp.int32)[None, :] <= q_idx[:, None]


def causal_block_sweep(block_fn, q):
    b, s = q.shape[0], q.shape[1]
    nb = s // Q_BLOCK
    qb = jnp.moveaxis(q.reshape((b, nb, Q_BLOCK) + q.shape[2:]), 1, 0)
    starts = jnp.arange(nb, dtype=jnp.int32) * Q_BLOCK
    out = lax.map(lambda a: block_fn(a[0], a[1]), (qb, starts))
    return jnp.moveaxis(out, 0, 1).reshape((b, s) + out.shape[3:])


def mla_branch(cq, ckv, kr, pos, q_norm_g, kv_norm_g, w_uq, w_ukv, qn_g, kn_g):
    b, s, _ = cq.shape
    q = (rms_norm(cq, q_norm_g) @ w_uq).reshape(b, s, MLA_HEADS, MLA_QK)
    kv = (rms_norm(ckv, kv_norm_g) @ w_ukv).reshape(b, s, MLA_HEADS, MLA_NOPE + MLA_V)
    k_nope, v = kv[..., :MLA_NOPE], kv[..., MLA_NOPE:]
    k_rope = jnp.broadcast_to(kr.reshape(b, s, 1, MLA_ROPE), (b, s, MLA_HEADS, MLA_ROPE))
    k = jnp.concatenate([k_nope, k_rope], axis=-1)
    q = rms_norm(q, qn_g)
    k = rms_norm(k, kn_g)
    q = jnp.concatenate([q[..., :MLA_NOPE], rope(q[..., MLA_NOPE:], pos)], axis=-1)
    k = jnp.concatenate([k[..., :MLA_NOPE], rope(k[..., MLA_NOPE:], pos)], axis=-1)
    scale = 1.0 / math.sqrt(MLA_QK)

    def block_fn(qblk, start):
        sc = jnp.einsum('bqhd,bkhd->bhqk', qblk, k).astype(jnp.float32) * scale
        sc = jnp.where(causal_mask(start, s)[None, None], sc, NEG_INF)
        p = jax.nn.softmax(sc, axis=-1).astype(v.dtype)
        return jnp.einsum('bhqk,bkhd->bqhd', p, v)

    o = causal_block_sweep(block_fn, q)
    return o.reshape(b, s, MLA_W)


def diff_branch(dq, dk, dv, pos, qn_g, kn_g, lam_vecs, subln_g, lam_init):
    b, s, _ = dq.shape
    q = rms_norm(dq.reshape(b, s, DIFF_HEADS * 2, DIFF_D), qn_g)
    k = rms_norm(dk.reshape(b, s, DIFF_HEADS * 2, DIFF_D), kn_g)
    q = rope(q, pos).reshape(b, s, DIFF_HEADS, 2, DIFF_D)
    k = rope(k, pos).reshape(b, s, DIFF_HEADS, 2, DIFF_D)
    v = dv.reshape(b, s, DIFF_HEADS, DIFF_V)
    lv = lam_vecs.astype(jnp.float32)
    lam = jnp.exp(jnp.sum(lv[0] * lv[1])) - jnp.exp(jnp.sum(lv[2] * lv[3])) + lam_init
    scale = 1.0 / math.sqrt(DIFF_D)

    def block_fn(qblk, start):
        sc = jnp.einsum('bqhmd,bkhmd->bhmqk', qblk, k).astype(jnp.float32) * scale
        sc = jnp.where(causal_mask(start, s)[None, None, None], sc, NEG_INF)
        p = jax.nn.softmax(sc, axis=-1)
        a = (p[:, :, 0] - lam * p[:, :, 1]).astype(v.dtype)
        return jnp.einsum('bhqk,bkhd->bqhd', a, v)

    o = causal_block_sweep(block_fn, q)
    o = rms_norm(o, subln_g) * (1.0 - lam_init)
    return o.reshape(b, s, DIFF_W)


def mem_branch(mq, mem, mem_norm_g, w_mem_kv, qn_g, kn_g):
    b, s, _ = mq.shape
    m = mem.shape[1]
    kv = rms_norm(mem, mem_norm_g) @ w_mem_kv
    k = rms_norm(kv[..., :MEM_W].reshape(b, m, MEM_HEADS, MEM_D), kn_g)
    v = kv[..., MEM_W:].reshape(b, m, MEM_HEADS, MEM_D)
    q = rms_norm(mq.reshape(b, s, MEM_HEADS, MEM_D), qn_g)
    sc = jnp.einsum('bshd,bmhd->bhsm', q, k).astype(jnp.float32) / math.sqrt(MEM_D)
    p = jax.nn.softmax(sc, axis=-1).astype(v.dtype)
    o = jnp.einsum('bhsm,bmhd->bshd', p, v)
    return o.reshape(b, s, MEM_W)


def setup_inputs(seed: int = 0) -> dict:
    key = jax.random.key(seed)
    ks = jax.random.split(key, 24)
    f32 = jnp.float32

    def nrm(k, shape, fan_in):
        return jax.random.normal(k, shape, f32) * (fan_in ** -0.5)

    def gain(k, shape):
        return 1.0 + 0.01 * jax.random.normal(k, shape, f32)

    x = jax.random.normal(ks[0], (BATCH, SEQ, D_MODEL), f32)
    mem = jax.random.normal(ks[1], (BATCH, MEM_TOKENS, D_MODEL), f32)
    offsets = jax.random.randint(ks[2], (BATCH, 1), 0, 4096, dtype=jnp.int32)
    positions = offsets + jnp.arange(SEQ, dtype=jnp.int32)[None, :]
    return {
        "x": x,
        "mem": mem,
        "positions": positions,
        "norm_g": gain(ks[3], (DEPTH, D_MODEL)),
        "w_in": nrm(ks[4], (DEPTH, D_MODEL, D_IN), D_MODEL),
        "mla_q_norm_g": gain(ks[5], (DEPTH, Q_LORA)),
        "mla_kv_norm_g": gain(ks[6], (DEPTH, KV_LORA)),
        "w_uq": nrm(ks[7], (DEPTH, Q_LORA, MLA_HEADS * MLA_QK), Q_LORA),
        "w_ukv": nrm(ks[8], (DEPTH, KV_LORA, MLA_HEADS * (MLA_NOPE + MLA_V)), KV_LORA),
        "mla_qn_g": gain(ks[9], (DEPTH, MLA_QK)),
        "mla_kn_g": gain(ks[10], (DEPTH, MLA_QK)),
        "diff_qn_g": gain(ks[11], (DEPTH, DIFF_D)),
        "diff_kn_g": gain(ks[12], (DEPTH, DIFF_D)),
        "diff_lambda": 0.1 * jax.random.normal(ks[13], (DEPTH, 4, DIFF_D), f32),
        "diff_subln_g": gain(ks[14], (DEPTH, DIFF_V)),
        "mem_norm_g": gain(ks[15], (DEPTH, D_MODEL)),
        "w_mem_kv": nrm(ks[16], (DEPTH, D_MODEL, 2 * MEM_W), D_MODEL),
        "mem_qn_g": gain(ks[17], (DEPTH, MEM_D)),
        "mem_kn_g": gain(ks[18], (DEPTH, MEM_D)),
        "w_out": nrm(ks[19], (DEPTH, D_MIX, D_MODEL), D_MIX),
    }


def reference(x, mem, positions, norm_g, w_in, mla_q_norm_g, mla_kv_norm_g, w_uq, w_ukv,
              mla_qn_g, mla_kn_g, diff_qn_g, diff_kn_g, diff_lambda, diff_subln_g,
              mem_norm_g, w_mem_kv, mem_qn_g, mem_kn_g, w_out):
    for l in range(DEPTH):
        lam_init = 0.8 - 0.6 * math.exp(-0.3 * l)
        h = rms_norm(x, norm_g[l])
        proj = h @ w_in[l]
        cq, ckv, kr, dq, dk, dv, mq, z = jnp.split(proj, IN_OFFSETS, axis=-1)
        y_mla = mla_branch(cq, ckv, kr, positions, mla_q_norm_g[l], mla_kv_norm_g[l],
                           w_uq[l], w_ukv[l], mla_qn_g[l], mla_kn_g[l])
        y_diff = diff_branch(dq, dk, dv, positions, diff_qn_g[l], diff_kn_g[l],
                             diff_lambda[l], diff_subln_g[l], lam_init)
        y_mem = mem_branch(mq, mem, mem_norm_g[l], w_mem_kv[l], mem_qn_g[l], mem_kn_g[l])
        y = jnp.concatenate([y_mla, y_diff, y_mem], axis=-1) * jax.nn.silu(z)
        x = x + y @ w_out[l]
    return x
```

```python
import math
import contextlib
import numpy as np
import ml_dtypes
import concourse.bass as bass
import concourse.mybir as mybir
from concourse.bass_utils import run_bass_kernel_spmd

F32 = mybir.dt.float32
BF16 = mybir.dt.bfloat16
I32 = mybir.dt.int32
ALU = mybir.AluOpType
AF = mybir.ActivationFunctionType
AX = mybir.AxisListType

ENGS = ("pe", "act", "dve", "pool", "sp")
EPS = 1e-6
DEPTH = 2


class Op:
    __slots__ = ("eng", "fn", "deps", "signal", "sigval", "semkey", "is_dma", "idx", "cc")

    def __init__(self, eng, fn, is_dma, semkey):
        self.cc = False
        self.eng = eng
        self.fn = fn
        self.deps = {}
        self.signal = False
        self.sigval = None
        self.semkey = semkey
        self.is_dma = is_dma
        self.idx = None


class Rec:
    def __init__(self, nc):
        self.nc = nc
        self.ops = {e: [] for e in ENGS}
        self.state = {}
        self.nops = 0
        self.last_dma = {}

    def _dep(self, op, prod, raw, war=False):
        if prod is op:
            return
        if prod.semkey == op.semkey and not prod.is_dma:
            if not (raw or war) or op.eng == "pe":
                return
        cur = op.deps.get(prod.semkey)
        if cur is None or cur.idx < prod.idx:
            op.deps[prod.semkey] = prod
        prod.signal = True

    def add(self, eng, name, args, reads=(), writes=(), is_dma=False, slot=None, cc=False):
        semkey = ("dma", slot) if is_dma else eng
        op = Op(eng, (name, args), is_dma, semkey)
        op.cc = cc
        op.idx = self.nops
        self.nops += 1
        if is_dma:
            op.signal = True
            prev = self.last_dma.get(semkey)
            if prev is not None:
                self._dep(op, prev, True)
            self.last_dma[semkey] = op
        reads = list(reads)
        writes = list(writes)
        ps_reads = [k for k in reads if isinstance(k, str) and k.startswith("ps")]
        if ps_reads:
            reads = [k for k in reads if k not in ps_reads]
            for k in ps_reads:
                st = self.state.setdefault(k, [{}, {}])
                for w in st[0].values():
                    self._dep(op, w, True)
            writes = writes + ps_reads
        for k in reads:
            st = self.state.setdefault(k, [{}, {}])
            for w in st[0].values():
                self._dep(op, w, True)
        for k in writes:
            st = self.state.setdefault(k, [{}, {}])
            for r in st[1].values():
                self._dep(op, r, False, war=True)
            for w in st[0].values():
                self._dep(op, w, False)
        for k in reads:
            self.state[k][1][semkey] = op
        for k in writes:
            st = self.state[k]
            if st[1]:
                st[0] = {}
                st[1] = {}
            st[0][semkey] = op
        self.ops[eng].append(op)
        return op

    def emit(self, final_waits=()):
        nc = self.nc
        counts = {}
        for e in ENGS:
            for op in self.ops[e]:
                if op.signal:
                    inc = 16 if (op.is_dma and not op.cc) else 1
                    counts[op.semkey] = counts.get(op.semkey, 0) + inc
                    op.sigval = counts[op.semkey]
        self.counts = counts
        print("[kernel] semaphores needed:", len(counts), "max count:", max(counts.values()))
        with contextlib.ExitStack() as es:
            sems = {}
            for i, k in enumerate(counts.keys()):
                sems[k] = es.enter_context(nc.semaphore("s%d" % i))
            block = es.enter_context(nc.Block())
            engmap = {"pe": block.tensor, "act": block.scalar, "dve": block.vector,
                      "pool": block.gpsimd, "sp": block.sync}

            def make(e):
                def body(eng):
                    known = {}
                    for op in self.ops[e]:
                        for sk, prod in op.deps.items():
                            v = prod.sigval
                            if known.get(sk, 0) < v:
                                eng.wait_ge(sems[sk], v)
                                known[sk] = v
                        ins = getattr(eng, op.fn[0])(*op.fn[1][0], **op.fn[1][1])
                        if op.signal:
                            if op.cc:
                                ins.then_inc(sems[op.semkey])
                            else:
                                ins.then_inc(sems[op.semkey], 16 if op.is_dma else 1)
                    if e == "sp":
                        for sk in final_waits:
                            if sk in counts:
                                eng.wait_ge(sems[sk], counts[sk])
                return body

            for e in ENGS:
                if self.ops[e] or e == "sp":
                    engmap[e](make(e))


def ARGS(*a, **k):
    return (a, k)


def gchunk(lc, j):
    return [j, 7 - j, 8 + j, 15 - j][lc]


WIN_K_COLS = [(256, 416), (672, 1184)]
WIN_Q_COLS = [(0, 256), (416, 672), (1184, 1440), (1440, 2464)]

KVS = {"km": [768, 512], "kd": [256, 512], "vm": [1024, 260], "vd": [512, 260]}
GROUPS = [[0, 1, 2, 3], [4, 5, 6, 7]]

PHASES = {"A0": [("A", 0)], "B0A1": [("B", 0), ("A", 1)], "B1": [("B", 1)],
          "F": [("A", 0), ("B", 0), ("A", 1), ("B", 1)]}


def build(mode):
    phases = PHASES[mode]
    fused = mode == "F"
    nc = bass.Bass("TRN2", target_bir_lowering=False)
    IN, OUT, INT = "ExternalInput", "ExternalOutput", "Internal"

    def dram(name, shape, dt, kind):
        if kind is None:
            return nc.dram_tensor(name, list(shape), dt).ap()
        return nc.dram_tensor(name, list(shape), dt, kind=kind).ap()

    prodB = {l for p, l in phases if p == "B"}
    prodA = {l for p, l in phases if p == "A"}
    xd = {}
    xd[0] = dram("xc", [4, 512, 1024], F32, IN) if (0 in prodA or 0 in prodB) else None
    if 0 in prodB:
        xd[1] = dram("x1", [4, 512, 1024], F32, None if fused else OUT)
    elif 1 in prodA or 1 in prodB:
        xd[1] = dram("x1", [4, 512, 1024], F32, IN)
    if 1 in prodB:
        xd[2] = dram("out", [4, 512, 1024], F32, OUT)
    pos_d = dram("pos", [4, 512], I32, IN)
    flg_d = dram("flg", [16], F32, IN)
    mem_d = dram("mem", [256, 1024], F32, IN)
    W = {}
    for name, shape in [("norm_g", [2, 1024]), ("w_in", [2, 1024, 2464]), ("mla_q_norm_g", [2, 256]),
                        ("mla_kv_norm_g", [2, 128]), ("w_uq", [2, 256, 768]), ("w_ukv", [2, 128, 1024]),
                        ("mla_qn_g", [2, 96]), ("mla_kn_g", [2, 96]), ("diff_qn_g", [2, 32]),
                        ("diff_kn_g", [2, 32]), ("diff_lambda", [2, 128]), ("diff_subln_g", [2, 64]),
                        ("mem_norm_g", [2, 1024]), ("w_mem_kv", [2, 1024, 512]), ("mem_qn_g", [2, 64]),
                        ("mem_kn_g", [2, 64]), ("w_out", [2, 1024, 1024])]:
        W[name] = dram(name, shape, F32, IN)
    shard = {}
    gath = {}
    for l in range(DEPTH):
        for nm, shp in KVS.items():
            for c in range(4):
                if l in prodA:
                    shard[(nm, l, c)] = dram("%s%d_%d" % (nm, l, c), shp, BF16, None if fused else OUT)
                if l in prodB:
                    gath[(nm, l, c)] = dram("%sg%d_%d" % (nm, l, c), [4 * shp[0], shp[1]], BF16, None if fused else IN)

    es = contextlib.ExitStack()
    with es:
        def sb(name, shape, dt):
            return es.enter_context(nc.sbuf_tensor(name, list(shape), dt))

        WinQ = sb("WinQ", [128, 8, 1792], BF16)
        Wuq = sb("Wuq", [128, 2, 768], BF16)
        WKO = sb("WKO", [128, 8192], BF16)
        Wout = WKO[:].rearrange("p (c n) -> p c n", c=8)
        WinK = WKO[:, 0:5376].rearrange("p (c n) -> p c n", c=8)
        Wukv = WKO[:, 5376:6400]
        Wmem = WKO[:, 0:4096].rearrange("p (c n) -> p c n", c=8)
        G = {}
        Gn = [sb("g_norm%d" % l_, [128, 1024], F32) for l_ in range(2)]
        for nm, n in [("mem_norm_g", 1024), ("mla_q_norm_g", 256), ("mla_kv_norm_g", 128),
                      ("mla_qn_g", 96), ("mla_kn_g", 96), ("diff_qn_g", 32), ("diff_kn_g", 32),
                      ("diff_lambda", 128), ("diff_subln_g", 64), ("mem_qn_g", 64), ("mem_kn_g", 64)]:
            G[nm] = sb("g_" + nm, [128, n], F32)
        identb = sb("identb", [128, 128], BF16)
        identf = sb("identf", [128, 128], F32)
        mhalf = sb("mhalf", [128, 8], F32)
        flg = sb("flg_sb", [128, 16], F32)
        posi = sb("posi", [128, 16], I32)
        posf = sb("posf", [128, 16], F32)
        invf = sb("invf", [128, 16], F32)
        cosT = sb("cosT", [128, 16, 16], F32)
        sinT = sb("sinT", [128, 16, 16], F32)
        lam_s = sb("lam_s", [128, 4], F32)
        nlam = sb("nlam", [128, 1], F32)
        sgs = sb("sgs", [128, 64], F32)
        hb = [sb("hb%d" % i, [128, 1024], BF16) for i in range(2)]
        T8 = sb("T8", [128, 8, 512], BF16)
        Ys = [sb("Y%d" % i, [128, 4, 1024], BF16) for i in range(2)]
        QTs = [sb("QT%d" % i, [96, 8, 512], BF16) for i in range(2)]
        DQcs = [sb("DQc%d" % i, [128, 2, 512], BF16) for i in range(2)]
        DQpad = [sb("DQpad%d" % i, [128, 512], BF16) for i in range(2)]
        KTst = sb("KTst", [96, 8, 512], BF16)
        DKT = sb("DKT", [128, 2, 512], BF16)
        rowmask = sb("rowmask", [128, 4], F32)
        onesf = sb("onesf", [128, 4], F32)
        MQTs = [sb("MQT%d" % i, [64, 4, 512], BF16) for i in range(2)]
        Vst = sb("Vst", [128, 8, 4, 65], BF16)
        VDst = sb("VDst", [128, 4, 4, 65], BF16)
        NKV = 2
        Kbuf = [sb("Kbuf%d" % i, [128, 2048], BF16) for i in range(NKV)]
        Vbuf = [sb("Vbuf%d" % i, [128, 4, 260], BF16) for i in range(NKV)]
        Kown = [sb("Kown%d" % i, [128, 512], BF16) for i in range(2)]
        Vown = [sb("Vown%d" % i, [128, 260], BF16) for i in range(2)]
        NPT = 3
        PT = [sb("PT%d" % i, [128, 2, 512], BF16) for i in range(NPT)]
        OT = [sb("OT%d" % i, [65, 512], F32) for i in range(1)]
        memKT = sb("memKT", [64, 4, 256], BF16)
        memV = sb("memV", [128, 2, 4, 65], BF16)
        class CX:
            pass
        cxs = []
        for u in range(2):
            cx = CX()
            cx.u = u
            cx.sq = sb("sq%d" % u, [128, 1024], F32)
            cx.ss = [sb("ss%d_%d" % (u, i), [128, 8], F32) for i in range(4)]
            cx.rs = [sb("rs%d_%d" % (u, i), [128, 8], F32) for i in range(4)]
            cx.f3 = [sb("f3_%d_%d" % (u, i), [128, 8, 96], F32) for i in range(2)]
            cx.r16 = [sb("r16_%d_%d" % (u, i), [128, 8, 16], F32) for i in range(4)]
            cx.c256 = sb("c256_%d" % u, [128, 256], BF16)
            cx.cT = sb("cT%d" % u, [128, 2, 128], BF16)
            cx.Kb = sb("Kb%d" % u, [128, 8, 96], BF16)
            cx.DKc = sb("DKc%d" % u, [128, 8, 32], BF16)
            cx.krs = sb("krs%d" % u, [128, 32], F32)
            cx.krr = sb("krr%d" % u, [128, 32], F32)
            cx.hb = hb[u]
            cx.xt = sb("xt%d" % u, [128, 1024], F32)
            cx.b = [0, 1] if u == 0 else [6, 7]
            cxs.append(cx)
        cxp = CX()
        cxp.u = "p"
        cxp.ss = [sb("pss", [128, 8], F32)]
        cxp.rs = [sb("prs", [128, 8], F32)]
        Dh = [sb("Dh%d" % i, [128, 4, 64], F32) for i in range(2)]
        rec = [sb("rec%d" % i, [128, 4], F32) for i in range(2)]

        psall = es.enter_context(nc.psum_tensor("psall", [128, 8, 512], F32))
        _sq1 = cxs[1].sq
        ang = _sq1[:, 0:256].rearrange("p (a b) -> p a b", a=16)
        angk = _sq1[:, 256:512].rearrange("p (a b) -> p a b", a=16)
        angi = _sq1[:, 512:768].rearrange("p (a b) -> p a b", a=16).bitcast(I32)

        class _Bank:
            def __init__(self, i):
                self.i = i

            def __getitem__(self, idx):
                return psall[:, self.i, :][idx]
        ps = [_Bank(i) for i in range(8)]
        R = Rec(nc)

        def psb(i):
            return psall[:, i, :].bitcast(BF16)

        def V(name, args, r=(), w=()):
            return R.add("dve", name, args, r, w)

        def A(name, args, r=(), w=()):
            return R.add("act", name, args, r, w)

        def P(name, args, r=(), w=()):
            return R.add("pool", name, args, r, w)

        def T(name, args, r=(), w=()):
            return R.add("pe", name, args, r, w)

        def DMA(name, args, r=(), w=(), slot=None, q="sp"):
            return R.add(q, name, args, r, w, is_dma=True, slot=slot)

        P("memset", ARGS(identb[:], 0.0), w=["identb"])
        P("memset", ARGS(identf[:], 0.0), w=["identf"])
        P("memset", ARGS(mhalf[:], -0.5), w=["mhalf"])
        P("affine_select", ARGS(out=identb[:], in_=identb[:], pattern=[[1, 128]], compare_op=ALU.not_equal,
                                    fill=1.0, base=0, channel_multiplier=-1), r=["identb"], w=["identb"])
        P("affine_select", ARGS(out=identf[:], in_=identf[:], pattern=[[1, 128]], compare_op=ALU.not_equal,
                                    fill=1.0, base=0, channel_multiplier=-1), r=["identf"], w=["identf"])
        P("memset", ARGS(onesf[:], 1.0), w=["onesf"])
        for m in range(4):
            P("affine_select", ARGS(out=rowmask[:, m:m + 1], in_=onesf[:, m:m + 1], pattern=[[0, 1]], compare_op=ALU.is_ge,
                                    fill=0.0, base=-32 * m, channel_multiplier=1), r=["onesf", "rowmask"], w=["rowmask"])
            P("affine_select", ARGS(out=rowmask[:, m:m + 1], in_=rowmask[:, m:m + 1], pattern=[[0, 1]], compare_op=ALU.is_ge,
                                    fill=0.0, base=32 * m + 31, channel_multiplier=-1), r=["rowmask"], w=["rowmask"])
        P("memset", ARGS(Vst[:], 1.0), w=["Vst"])
        P("memset", ARGS(VDst[:], 1.0), w=["VDst"])
        P("memset", ARGS(memV[:], 1.0), w=["memV"])
        DMA("dma_start", ARGS(out=flg[:], in_=flg_d.partition_broadcast(128)), w=["flg"], slot="flg")
        DMA("dma_start", ARGS(out=posi[:].rearrange("p (c t) -> p c t", c=4),
                                  in_=pos_d.rearrange("c (p t) -> p c t", t=4)), w=["posi"], slot="posi")
        V("tensor_copy", ARGS(out=posf[:], in_=posi[:]), r=["posi"], w=["posf"])
        for i in range(16):
            val = float(np.float32(10000.0 ** (-(2.0 * i) / 32.0)))
            P("memset", ARGS(invf[:, i:i + 1], val), w=["invf"])
        V("tensor_tensor", ARGS(out=ang, in0=posf[:].unsqueeze(2).to_broadcast([128, 16, 16]),
                                    in1=invf[:].unsqueeze(1).to_broadcast([128, 16, 16]), op=ALU.mult),
          r=["posf", "invf"], w=[("sq", 1)])
        TWO_PI = 2.0 * math.pi
        c1 = 6.28125
        c2 = float(np.round((TWO_PI - c1) * 2 ** 20) / 2 ** 20)
        c3 = float(TWO_PI - c1 - c2)
        V("tensor_scalar", ARGS(out=angi, in0=ang, scalar1=1.0 / TWO_PI, scalar2=None, op0=ALU.mult),
          r=[("sq", 1)], w=[("sq", 1)])
        V("tensor_copy", ARGS(out=angk, in_=angi), r=[("sq", 1)], w=[("sq", 1)])
        for cc in (c1, c2, c3):
            V("scalar_tensor_tensor", ARGS(out=ang, in0=angk, scalar=-cc, in1=ang,
                                                      op0=ALU.mult, op1=ALU.add), r=[("sq", 1)], w=[("sq", 1)])
        PI_IN = 3.1415925
        V("tensor_scalar", ARGS(out=sinT[:], in0=ang, scalar1=PI_IN, scalar2=-PI_IN, op0=ALU.min, op1=ALU.max),
          r=[("sq", 1)], w=["sinT"])
        V("tensor_scalar", ARGS(out=cosT[:], in0=ang, scalar1=math.pi / 2, scalar2=None, op0=ALU.add),
          r=[("sq", 1)], w=["cosT"])
        V("tensor_scalar", ARGS(out=angk, in0=cosT[:], scalar1=math.pi, scalar2=None, op0=ALU.is_gt),
          r=["cosT"], w=[("sq", 1)])
        V("scalar_tensor_tensor", ARGS(out=cosT[:], in0=angk, scalar=-TWO_PI, in1=cosT[:],
                                           op0=ALU.mult, op1=ALU.add), r=[("sq", 1), "cosT"], w=["cosT"])
        V("tensor_scalar", ARGS(out=cosT[:], in0=cosT[:], scalar1=PI_IN, scalar2=-PI_IN, op0=ALU.min, op1=ALU.max),
          r=["cosT"], w=["cosT"])
        A("activation", ARGS(out=sinT[:], in_=sinT[:], func=AF.Sin), r=["sinT"], w=["sinT"])
        A("activation", ARGS(out=cosT[:], in_=cosT[:], func=AF.Sin), r=["cosT"], w=["cosT"])

        def NG(l):
            return "norm_g"

        def interleave(gens):
            gens = list(gens)
            while gens:
                for g in list(gens):
                    try:
                        next(g)
                    except StopIteration:
                        gens.remove(g)

        def pad(n):
            for _ in range(n):
                yield

        def run(g):
            for _ in g:
                pass

        def rstd_of(cx, ssum_ap, n, Dn, i, rk):
            V("tensor_scalar", ARGS(out=ssum_ap, in0=ssum_ap, scalar1=1.0 / Dn, scalar2=EPS,
                                    op0=ALU.mult, op1=ALU.add), r=[rk], w=[rk])
            yield
            P("tensor_tensor", ARGS(out=cx.rs[i][:, 0:n], in0=ssum_ap, in1=mhalf[:, 0:n], op=ALU.pow),
              r=[rk, "mhalf"], w=[("rs", cx.u, i)])
            yield
            return cx.rs[i][:, 0:n]

        def wl(dst, src, key):
            DMA("dma_start", ARGS(out=dst, in_=src), w=[key], slot=key, q="pool")

        def gl(nm, l):
            DMA("dma_start", ARGS(out=G[nm][:], in_=W[nm][l].partition_broadcast(128)), w=["g_" + nm], slot="g_" + nm)

        wko = [None]

        def ensure_wko(what, l):
            if wko[0] == (what, l):
                return
            wko[0] = (what, l)
            if what == "K":
                win = W["w_in"][l].rearrange("(c p) n -> p c n", p=128)
                off = 0
                for i, (a_, b_) in enumerate(WIN_K_COLS):
                    DMA("dma_start", ARGS(out=WinK[:, :, off:off + (b_ - a_)], in_=win[:, :, a_:b_]), w=["WKO"],
                        slot=("WKO", i), q="pool")
                    off += b_ - a_
                DMA("dma_start", ARGS(out=Wukv, in_=W["w_ukv"][l]), w=["WKO"], slot=("WKO", 2), q="pool")
            elif what == "O":
                DMA("dma_start", ARGS(out=Wout, in_=W["w_out"][l].rearrange("(c p) n -> p c n", p=128)), w=["WKO"],
                    slot=("WKO", 0), q="pool")
            else:
                DMA("dma_start", ARGS(out=Wmem, in_=W["w_mem_kv"][l].rearrange("(c p) n -> p c n", p=128)), w=["WKO"],
                    slot=("WKO", 0), q="pool")

        def load_Kg(l):
            for nm in ("mla_kv_norm_g", "mla_kn_g", "diff_kn_g"):
                gl(nm, l)

        def load_Q(l):
            win = W["w_in"][l].rearrange("(c p) n -> p c n", p=128)
            off = 0
            for i, (a_, b_) in enumerate(WIN_Q_COLS):
                wl(WinQ[:, :, off:off + (b_ - a_)], win[:, :, a_:b_], "WinQ%d" % i)
                off += b_ - a_
            wl(Wuq[:], W["w_uq"][l].rearrange("(c p) n -> p c n", p=128), "Wuq")
            for nm in ("mla_q_norm_g", "mla_qn_g", "diff_qn_g", "mem_qn_g"):
                gl(nm, l)

        def load_M(l):
            for nm in ("mem_norm_g", "mem_kn_g"):
                gl(nm, l)

        def load_P(l):
            for nm in ("diff_lambda", "diff_subln_g"):
                gl(nm, l)
            lam_init = 0.8 - 0.6 * math.exp(-0.3 * l)
            lv = G["diff_lambda"][:].rearrange("p (a b d) -> p a b d", a=2, b=2)
            V("tensor_tensor", ARGS(out=cxs[0].sq[:, 0:64].rearrange("p (a d) -> p a d", a=2), in0=lv[:, :, 0, :],
                                    in1=lv[:, :, 1, :], op=ALU.mult), r=["g_diff_lambda"], w=[("sq", 0)])
            V("tensor_reduce", ARGS(out=lam_s[:, 0:2], in_=cxs[0].sq[:, 0:64].rearrange("p (a d) -> p a d", a=2),
                                    axis=AX.X, op=ALU.add), r=[("sq", 0)], w=["lam_s"])
            A("activation", ARGS(out=lam_s[:, 2:4], in_=lam_s[:, 0:2], func=AF.Exp), r=["lam_s"], w=["lam_s2"])
            V("tensor_tensor", ARGS(out=nlam[:], in0=lam_s[:, 3:4], in1=lam_s[:, 2:3], op=ALU.subtract),
              r=["lam_s2"], w=["nlam"])
            V("tensor_scalar", ARGS(out=nlam[:], in0=nlam[:], scalar1=-lam_init, scalar2=None, op0=ALU.add),
              r=["nlam"], w=["nlam"])
            V("tensor_scalar", ARGS(out=sgs[:], in0=G["diff_subln_g"][:], scalar1=1.0 - lam_init, scalar2=None,
                                    op0=ALU.mult), r=["g_diff_subln_g"], w=["sgs"])

        WK = ["WKO", "WKO"]
        WQ = ["WinQ0", "WinQ1", "WinQ2", "WinQ3"]

        WK = ["WKO", "WKO"]
        WQ = ["WinQ0", "WinQ1", "WinQ2", "WinQ3"]

        def norm_transpose(cx, gap, gkey, src_ap, srck, dstT, dstk, col0):
            u, b0 = cx.u, cx.b[0]
            yield from pad(PADA)
            A("activation", ARGS(out=cx.sq[:], in_=src_ap, func=AF.Square, accum_out=cx.ss[0][:, 0:1]),
              r=[srck], w=[("sq", u), ("ss", u, 0)])
            yield
            rr = yield from rstd_of(cx, cx.ss[0][:, 0:1], 1, 1024.0, 0, ("ss", u, 0))
            for hf in range(2):
                hs = slice(hf * 512, (hf + 1) * 512)
                V("scalar_tensor_tensor", ARGS(out=cx.hb[:, hs], in0=src_ap[:, hs], scalar=rr, in1=gap[:, hs],
                                               op0=ALU.mult, op1=ALU.mult),
                  r=[srck, ("rs", u, 0), gkey], w=[("hb", u)])
                yield
            yield from pad(PADP)
            for kc in range(8):
                T("transpose", ARGS(out=psb(b0)[:, kc * 128:(kc + 1) * 128],
                                    in_=cx.hb[:, kc * 128:(kc + 1) * 128], identity=identb[:]),
                  r=[("hb", u), "identb"], w=["ps%d" % b0])
                if kc % 4 == 3:
                    yield
            for hf in range(2):
                V("tensor_copy", ARGS(out=dstT[:, 4 * hf:4 * hf + 4, col0:col0 + 128],
                                      in_=psb(b0)[:, 512 * hf:512 * hf + 512].rearrange("p (k n) -> p k n", k=4)),
                  r=["ps%d" % b0], w=[dstk])
                yield

        def rope(cx, src3, srck, dst3, dstk, lo, ct, nh):
            u = cx.u
            x1 = src3[:, :, lo:lo + 16]
            x2 = src3[:, :, lo + 16:lo + 32]
            cs = cosT[:, ct, :].unsqueeze(1).to_broadcast([128, nh, 16])
            sn = sinT[:, ct, :].unsqueeze(1).to_broadcast([128, nh, 16])
            a, b, c_, d = [cx.r16[k][:, 0:nh, :] for k in range(4)]
            V("tensor_tensor", ARGS(out=a, in0=x1, in1=cs, op=ALU.mult), r=[srck, "cosT"], w=[("r16", u, 0)])
            yield
            V("tensor_tensor", ARGS(out=b, in0=x2, in1=sn, op=ALU.mult), r=[srck, "sinT"], w=[("r16", u, 1)])
            yield
            V("tensor_tensor", ARGS(out=c_, in0=x2, in1=cs, op=ALU.mult), r=[srck, "cosT"], w=[("r16", u, 2)])
            yield
            V("tensor_tensor", ARGS(out=d, in0=x1, in1=sn, op=ALU.mult), r=[srck, "sinT"], w=[("r16", u, 3)])
            yield
            V("tensor_tensor", ARGS(out=dst3[:, :, lo:lo + 16], in0=a, in1=b, op=ALU.subtract),
              r=[("r16", u, 0), ("r16", u, 1)], w=[dstk])
            yield
            V("tensor_tensor", ARGS(out=dst3[:, :, lo + 16:lo + 32], in0=c_, in1=d, op=ALU.add),
              r=[("r16", u, 2), ("r16", u, 3)], w=[dstk])
            yield

        def head_norm(cx, src3, srck, nh, Dh_, gname, si, dst3, dstk, scale=None):
            u = cx.u
            sqv = cx.sq[:, 0:nh * Dh_].rearrange("p (h d) -> p h d", h=nh)
            yield from pad(PADA)
            A("activation", ARGS(out=sqv, in_=src3, func=AF.Square), r=[srck], w=[("sq", u)])
            yield
            V("tensor_reduce", ARGS(out=cx.ss[si][:, 0:nh], in_=sqv, axis=AX.X, op=ALU.add),
              r=[("sq", u)], w=[("ss", u, si)])
            yield
            rr = yield from rstd_of(cx, cx.ss[si][:, 0:nh], nh, float(Dh_), si, ("ss", u, si))
            if scale is not None:
                V("tensor_scalar", ARGS(out=rr, in0=rr, scalar1=float(scale), scalar2=None, op0=ALU.mult),
                  r=[("rs", u, si)], w=[("rs", u, si)])
                yield
            V("tensor_tensor", ARGS(out=dst3, in0=src3, in1=G[gname][:, 0:Dh_].unsqueeze(1).to_broadcast([128, nh, Dh_]),
                                    op=ALU.mult), r=[srck, "g_" + gname], w=[dstk])
            yield
            V("tensor_tensor", ARGS(out=dst3, in0=rr.unsqueeze(2).to_broadcast([128, nh, Dh_]), in1=dst3,
                                    op=ALU.mult), r=[("rs", u, si), dstk], w=[dstk])
            yield

        def diff_pack(cx, ct):
            u, b0 = cx.u, cx.b[0]
            yield from rope(cx, cx.f3[1], ("f3", u, 1), cx.f3[1], ("f3", u, 1), 0, ct, 8)
            V("tensor_copy", ARGS(out=cx.DKc[:], in_=cx.f3[1][:, :, 0:32]), r=[("f3", u, 1)], w=[("DKc", u)])
            yield
            yield from pad(PADP)
            for g in range(2):
                T("transpose", ARGS(out=psb(b0)[:, g * 128:(g + 1) * 128],
                                    in_=cx.DKc[:, 4 * g:4 * g + 4, :].rearrange("p m d -> p (m d)"),
                                    identity=identb[:]), r=[("DKc", u), "identb"], w=["ps%d" % b0])
            yield

        def x_tile_src(l, c, t):
            return xd[l][c].rearrange("(p t) f -> p t f", t=4)[:, t, :]

        def A_tile(l, c, t, cx):
            u = cx.u
            b0, b1 = cx.b
            B1 = "ps%d" % b1
            B0 = "ps%d" % b0
            ct = c * 4 + t
            tsl = slice(t * 128, (t + 1) * 128)
            DMA("dma_start", ARGS(out=cx.xt[:], in_=x_tile_src(l, c, t)), r=[("xd", l, c, t)], w=[("xt", u)], slot=("xt", u))
            for _ in range(PADX):
                yield
            yield from norm_transpose(cx, Gn[l][:], "g_norm%d" % l, cx.xt[:], ("xt", u), T8, ("T8", t), t * 128)
            yield from pad(PADP)
            for kc in range(8):
                T("matmul", ARGS(ps[b1][:], lhsT=T8[:, kc, tsl], rhs=WinK[:, kc, 160:672],
                                 start=(kc == 0), stop=(kc == 7)), r=[("T8", t), "WKO"], w=[B1])
            yield
            dk3 = ps[b1][:, 0:256].rearrange("p (m d) -> p m d", m=8)
            yield from head_norm(cx, dk3, B1, 8, 32, "diff_kn_g", 0, cx.f3[1][:, :, 0:32], ("f3", u, 1))
            V("tensor_copy", ARGS(out=VDst[:, :, t, 0:64], in_=ps[b1][:, 256:512].rearrange("p (h d) -> p h d", h=4)),
              r=[B1], w=["VDst"])
            yield
            yield from pad(PADP)
            for kc in range(8):
                T("matmul", ARGS(ps[b1][:, 0:160], lhsT=T8[:, kc, tsl], rhs=WinK[:, kc, 0:160],
                                 start=(kc == 0), stop=(kc == 7)), r=[("T8", t), "WKO"], w=[B1])
            yield
            yield from pad(PADA)
            A("activation", ARGS(out=cx.sq[:, 0:128], in_=ps[b1][:, 0:128], func=AF.Square,
                                 accum_out=cx.ss[2][:, 0:1]), r=[B1], w=[("sq", u), ("ss", u, 2)])
            yield
            rr = yield from rstd_of(cx, cx.ss[2][:, 0:1], 1, 128.0, 2, ("ss", u, 2))
            V("scalar_tensor_tensor", ARGS(out=cx.c256[:, 0:128], in0=ps[b1][:, 0:128], scalar=rr,
                                           in1=G["mla_kv_norm_g"][:], op0=ALU.mult, op1=ALU.mult),
              r=[B1, ("rs", u, 2), "g_mla_kv_norm_g"], w=[("c256", u)])
            yield
            T("transpose", ARGS(out=psb(b0)[:, 0:128], in_=cx.c256[:, 0:128], identity=identb[:]),
              r=[("c256", u), "identb"], w=[B0])
            yield
            V("tensor_copy", ARGS(out=cx.cT[:, 0, :], in_=psb(b0)[:, 0:128]), r=[B0], w=[("cT", u)])
            yield
            yield from pad(PADA)
            A("activation", ARGS(out=cx.krs[:], in_=ps[b1][:, 128:160], func=AF.Square,
                                 accum_out=cx.ss[3][:, 0:1]), r=[B1], w=[("krs", u), ("ss", u, 3)])
            yield
            V("tensor_tensor", ARGS(out=cx.krs[:], in0=ps[b1][:, 128:160], in1=G["mla_kn_g"][:, 64:96], op=ALU.mult),
              r=[B1, "g_mla_kn_g"], w=[("krs", u)])
            yield
            yield from rope(cx, cx.krs[:].unsqueeze(1), ("krs", u), cx.krr[:].unsqueeze(1), ("krr", u), 0, ct, 1)
            for n in range(2):
                T("matmul", ARGS(ps[b1][:], lhsT=cx.cT[:, 0, :], rhs=Wukv[:, n * 512:(n + 1) * 512],
                                 start=True, stop=True), r=[("cT", u), "WKO"], w=[B1])
                yield
                kvv = ps[b1][:].rearrange("p (h d) -> p h d", h=4)
                yield from pad(PADA)
                A("activation", ARGS(out=cx.sq[:, n * 256:(n + 1) * 256].rearrange("p (h d) -> p h d", h=4),
                                     in_=kvv[:, :, 0:64], func=AF.Square), r=[B1], w=[("sq", u)])
                yield
                V("tensor_tensor", ARGS(out=cx.f3[0][:, n * 4:(n + 1) * 4, 0:64], in0=kvv[:, :, 0:64],
                                        in1=G["mla_kn_g"][:, 0:64].unsqueeze(1).to_broadcast([128, 4, 64]), op=ALU.mult),
                  r=[B1, "g_mla_kn_g"], w=[("f3", u, 0)])
                yield
                V("tensor_copy", ARGS(out=Vst[:, n * 4:(n + 1) * 4, t, 0:64], in_=kvv[:, :, 64:128]),
                  r=[B1], w=["Vst"])
                yield
            V("tensor_reduce", ARGS(out=cx.ss[1][:, 0:8], in_=cx.sq[:, 0:512].rearrange("p (h d) -> p h d", h=8),
                                    axis=AX.X, op=ALU.add), r=[("sq", u)], w=[("ss", u, 1)])
            yield
            V("tensor_scalar", ARGS(out=cx.ss[1][:, 0:8], in0=cx.ss[1][:, 0:8], scalar1=cx.ss[3][:, 0:1], scalar2=None,
                                    op0=ALU.add), r=[("ss", u, 1), ("ss", u, 3)], w=[("ss", u, 1)])
            yield
            rk = yield from rstd_of(cx, cx.ss[1][:, 0:8], 8, 96.0, 1, ("ss", u, 1))
            V("tensor_tensor", ARGS(out=cx.Kb[:, :, 0:64], in0=rk.unsqueeze(2).to_broadcast([128, 8, 64]),
                                    in1=cx.f3[0][:, :, 0:64], op=ALU.mult), r=[("rs", u, 1), ("f3", u, 0)], w=[("Kb", u)])
            yield
            V("tensor_tensor", ARGS(out=cx.Kb[:, :, 64:96], in0=rk.unsqueeze(2).to_broadcast([128, 8, 32]),
                                    in1=cx.krr[:].unsqueeze(1).to_broadcast([128, 8, 32]), op=ALU.mult),
              r=[("rs", u, 1), ("krr", u)], w=[("Kb", u)])
            yield
            yield from pad(PADP)
            for h in range(8):
                T("transpose", ARGS(out=psb(b0)[0:96, h * 128:(h + 1) * 128], in_=cx.Kb[:, h, :],
                                    identity=identb[:]), r=[("Kb", u), "identb"], w=[B0])
                if h % 4 == 3:
                    yield
            for hf in range(2):
                V("tensor_copy", ARGS(out=KTst[:, 4 * hf:4 * hf + 4, tsl],
                                      in_=psb(b0)[0:96, 512 * hf:512 * hf + 512].rearrange("p (h n) -> p h n", h=4)),
                  r=[B0], w=["KTst"])
                yield
            yield from diff_pack(cx, ct)
            V("tensor_copy", ARGS(out=DKT[:, :, tsl], in_=psb(b0)[:, 0:256].rearrange("p (g n) -> p g n", g=2)),
              r=[B0], w=["DKT"])
            yield

        def kv_cc(l, c, nm):
            R.add("pool", "collective_compute",
                  ARGS("AllGather", ALU.bypass, replica_groups=GROUPS,
                       ins=[shard[(nm, l, c)].opt()], outs=[gath[(nm, l, c)].opt()]),
                  reads=[("kvs", nm, l, c)], writes=[("kvg", nm, l, c)],
                  is_dma=True, slot=("cc", nm), cc=True)

        def kv_store(l, c, cc_now=True):
            if True:
                DMA("dma_start", ARGS(out=shard[("km", l, c)].rearrange("(h d) n -> d h n", h=8), in_=KTst[:]),
                    r=["KTst"], w=[("kvs", "km", l, c)], slot=("st", 0))
                DMA("dma_start", ARGS(out=shard[("kd", l, c)].rearrange("(g r) n -> r g n", g=2), in_=DKT[:]),
                    r=["DKT"], w=[("kvs", "kd", l, c)], slot=("st", 1))
                DMA("dma_start", ARGS(out=shard[("vm", l, c)].rearrange("(h p) n -> p h n", h=8),
                                      in_=Vst[:].rearrange("p h t d -> p h (t d)")),
                    r=["Vst"], w=[("kvs", "vm", l, c)], slot=("st", 2))
                DMA("dma_start", ARGS(out=shard[("vd", l, c)].rearrange("(h p) n -> p h n", h=4),
                                      in_=VDst[:].rearrange("p h t d -> p h (t d)")),
                    r=["VDst"], w=[("kvs", "vd", l, c)], slot=("st", 3))
                if fused and cc_now:
                    for nm in ("km", "kd", "vm", "vd"):
                        R.add("pool", "collective_compute",
                              ARGS("AllGather", ALU.bypass, replica_groups=GROUPS,
                                   ins=[shard[(nm, l, c)].opt()], outs=[gath[(nm, l, c)].opt()]),
                              reads=[("kvs", nm, l, c)], writes=[("kvg", nm, l, c)],
                              is_dma=True, slot=("cc", nm), cc=True)

        def M_tile(l, mt, cx):
            u = cx.u
            b0, b1 = cx.b
            DMA("dma_start", ARGS(out=cx.xt[:], in_=mem_d[mt * 128:(mt + 1) * 128, :]), w=[("xt", u)], slot=("xt", u))
            yield
            yield from norm_transpose(cx, G["mem_norm_g"][:], "g_mem_norm_g", cx.xt[:], ("xt", u), T8, ("T8", mt), mt * 128)
            msl = slice(mt * 128, (mt + 1) * 128)
            yield from pad(PADP)
            for kc in range(8):
                T("matmul", ARGS(ps[b1][:], lhsT=T8[:, kc, msl], rhs=Wmem[:, kc, :],
                                 start=(kc == 0), stop=(kc == 7)), r=[("T8", mt), "WKO"], w=["ps%d" % b1])
            yield
            k3 = ps[b1][:, 0:256].rearrange("p (h d) -> p h d", h=4)
            yield from head_norm(cx, k3, "ps%d" % b1, 4, 64, "mem_kn_g", 0, cx.f3[1][:, 0:4, 0:64], ("f3", u, 1))
            V("tensor_copy", ARGS(out=cx.Kb[:, 0:4, 0:64], in_=cx.f3[1][:, 0:4, 0:64]), r=[("f3", u, 1)], w=[("Kb", u)])
            yield
            yield from pad(PADP)
            for h in range(4):
                T("transpose", ARGS(out=psb(b0)[0:64, h * 128:(h + 1) * 128], in_=cx.Kb[:, h, 0:64],
                                    identity=identb[:]), r=[("Kb", u), "identb"], w=["ps%d" % b0])
            yield
            V("tensor_copy", ARGS(out=memKT[:, :, msl], in_=psb(b0)[0:64, 0:512].rearrange("p (h n) -> p h n", h=4)),
              r=["ps%d" % b0], w=["memKT"])
            yield
            V("tensor_copy", ARGS(out=memV[:, mt, :, 0:64], in_=ps[b1][:, 256:512].rearrange("p (h d) -> p h d", h=4)),
              r=["ps%d" % b1], w=["memV"])
            yield

        def mem_kv(l):
            ensure_wko("M", l)
            interleave([M_tile(l, 0, cxs[0]), M_tile(l, 1, cxs[1])])

        def Q_tile(l, c, t, cx):
            u = cx.u
            b0, b1 = cx.b
            B1 = "ps%d" % b1
            B0 = "ps%d" % b0
            ct = c * 4 + t
            par = c % 2
            Y, QT, MQT, DQc = Ys[par], QTs[par], MQTs[par], DQcs[par]
            tsl = slice(t * 128, (t + 1) * 128)
            DMA("dma_start", ARGS(out=cx.xt[:], in_=x_tile_src(l, c, t)), r=[("xd", l, c, t)], w=[("xt", u)], slot=("xt", u))
            for _ in range(PADX):
                yield
            yield from norm_transpose(cx, Gn[l][:], "g_norm%d" % l, cx.xt[:], ("xt", u), T8, ("T8", t), t * 128)
            for kc in range(8):
                T("matmul", ARGS(ps[b1][:], lhsT=T8[:, kc, tsl], rhs=WinQ[:, kc, 0:512],
                                 start=(kc == 0), stop=(kc == 7)), r=[("T8", t), "WinQ0", "WinQ1"], w=[B1])
            yield
            yield from pad(PADA)
            A("activation", ARGS(out=cx.sq[:, 0:256], in_=ps[b1][:, 0:256], func=AF.Square,
                                 accum_out=cx.ss[2][:, 0:1]), r=[B1], w=[("sq", u), ("ss", u, 2)])
            yield
            rr = yield from rstd_of(cx, cx.ss[2][:, 0:1], 1, 256.0, 2, ("ss", u, 2))
            V("scalar_tensor_tensor", ARGS(out=cx.c256[:], in0=ps[b1][:, 0:256], scalar=rr,
                                           in1=G["mla_q_norm_g"][:], op0=ALU.mult, op1=ALU.mult),
              r=[B1, ("rs", u, 2), "g_mla_q_norm_g"], w=[("c256", u)])
            yield
            yield from pad(PADP)
            for j in range(2):
                T("transpose", ARGS(out=psb(b0)[:, j * 128:(j + 1) * 128], in_=cx.c256[:, j * 128:(j + 1) * 128],
                                    identity=identb[:]), r=[("c256", u), "identb"], w=[B0])
            yield
            V("tensor_copy", ARGS(out=cx.cT[:], in_=psb(b0)[:, 0:256].rearrange("p (j n) -> p j n", j=2)),
              r=[B0], w=[("cT", u)])
            yield
            dq3 = ps[b1][:, 256:512].rearrange("p (m d) -> p m d", m=8)
            yield from head_norm(cx, dq3, B1, 8, 32, "diff_qn_g", 0, cx.f3[1][:, :, 0:32], ("f3", u, 1),
                                 scale=1.0 / math.sqrt(32.0))
            yield from diff_pack(cx, ct)
            V("tensor_copy", ARGS(out=DQc[:, :, tsl], in_=psb(b0)[:, 0:256].rearrange("p (g n) -> p g n", g=2)),
              r=[B0], w=[("DQc", par)])
            yield
            yield from pad(PADP)
            for kc in range(8):
                T("matmul", ARGS(ps[b1][:, 0:256], lhsT=T8[:, kc, tsl], rhs=WinQ[:, kc, 512:768],
                                 start=(kc == 0), stop=(kc == 7)), r=[("T8", t), "WinQ2"], w=[B1])
            yield
            mq3 = ps[b1][:, 0:256].rearrange("p (h d) -> p h d", h=4)
            yield from head_norm(cx, mq3, B1, 4, 64, "mem_qn_g", 3, cx.f3[1][:, 0:4, 0:64], ("f3", u, 1), scale=0.125)
            V("tensor_copy", ARGS(out=cx.Kb[:, 0:4, 0:64], in_=cx.f3[1][:, 0:4, 0:64]), r=[("f3", u, 1)], w=[("Kb", u)])
            yield
            yield from pad(PADP)
            for h in range(4):
                T("transpose", ARGS(out=psb(b0)[0:64, h * 128:(h + 1) * 128], in_=cx.Kb[:, h, 0:64],
                                    identity=identb[:]), r=[("Kb", u), "identb"], w=[B0])
            yield
            V("tensor_copy", ARGS(out=MQT[:, :, tsl], in_=psb(b0)[0:64, 0:512].rearrange("p (h n) -> p h n", h=4)),
              r=[B0], w=[("MQT", par)])
            yield
            for n in range(2):
                yield from pad(PADP)
                for kc in range(8):
                    T("matmul", ARGS(ps[b1][:], lhsT=T8[:, kc, tsl], rhs=WinQ[:, kc, 768 + n * 512:768 + (n + 1) * 512],
                                     start=(kc == 0), stop=(kc == 7)), r=[("T8", t), "WinQ3"], w=[B1])
                yield
                th = cx.sq[:, n * 512:(n + 1) * 512]
                yield from pad(PADA)
                A("activation", ARGS(out=th, in_=ps[b1][:], func=AF.Tanh, scale=0.5), r=[B1], w=[("sq", u)])
                yield
                V("tensor_scalar", ARGS(out=th, in0=th, scalar1=0.5, scalar2=0.5, op0=ALU.mult, op1=ALU.add),
                  r=[("sq", u)], w=[("sq", u)])
                yield
                V("tensor_tensor", ARGS(out=Y[:, t, n * 512:(n + 1) * 512], in0=ps[b1][:], in1=th, op=ALU.mult),
                  r=[B1, ("sq", u)], w=[("Y", par, t)])
                yield
            for n in range(2):
                yield from pad(PADP)
                for j in range(2):
                    T("matmul", ARGS(ps[b1][:, 0:384], lhsT=cx.cT[:, j, :], rhs=Wuq[:, j, n * 384:(n + 1) * 384],
                                     start=(j == 0), stop=(j == 1)), r=[("cT", u), "Wuq"], w=[B1])
                yield
                q3 = ps[b1][:, 0:384].rearrange("p (h d) -> p h d", h=4)
                yield from pad(PADA)
                A("activation", ARGS(out=cx.sq[:, n * 384:(n + 1) * 384].rearrange("p (h d) -> p h d", h=4),
                                     in_=q3, func=AF.Square), r=[B1], w=[("sq", u)])
                yield
                V("tensor_tensor", ARGS(out=cx.f3[0][:, n * 4:(n + 1) * 4, :], in0=q3,
                                        in1=G["mla_qn_g"][:].unsqueeze(1).to_broadcast([128, 4, 96]),
                                        op=ALU.mult), r=[B1, "g_mla_qn_g"], w=[("f3", u, 0)])
                yield
            V("tensor_reduce", ARGS(out=cx.ss[1][:, 0:8], in_=cx.sq[:, 0:768].rearrange("p (h d) -> p h d", h=8),
                                    axis=AX.X, op=ALU.add), r=[("sq", u)], w=[("ss", u, 1)])
            yield
            rq = yield from rstd_of(cx, cx.ss[1][:, 0:8], 8, 96.0, 1, ("ss", u, 1))
            V("tensor_scalar", ARGS(out=rq, in0=rq, scalar1=1.0 / math.sqrt(96.0), scalar2=None, op0=ALU.mult),
              r=[("rs", u, 1)], w=[("rs", u, 1)])
            yield
            V("tensor_tensor", ARGS(out=cx.f3[0][:], in0=rq.unsqueeze(2).to_broadcast([128, 8, 96]),
                                    in1=cx.f3[0][:], op=ALU.mult), r=[("rs", u, 1), ("f3", u, 0)], w=[("f3", u, 0)])
            yield
            V("tensor_copy", ARGS(out=cx.Kb[:, :, 0:64], in_=cx.f3[0][:, :, 0:64]), r=[("f3", u, 0)], w=[("Kb", u)])
            yield
            yield from rope(cx, cx.f3[0], ("f3", u, 0), cx.Kb, ("Kb", u), 64, ct, 8)
            yield from pad(PADP)
            for h in range(8):
                T("transpose", ARGS(out=psb(b0)[0:96, h * 128:(h + 1) * 128], in_=cx.Kb[:, h, :],
                                    identity=identb[:]), r=[("Kb", u), "identb"], w=[B0])
                if h % 4 == 3:
                    yield
            for hf in range(2):
                V("tensor_copy", ARGS(out=QT[:, 4 * hf:4 * hf + 4, tsl],
                                      in_=psb(b0)[0:96, 512 * hf:512 * hf + 512].rearrange("p (h n) -> p h n", h=4)),
                  r=[B0], w=[("QT", par)])
                yield

        def O_tile(l, c, t, cx):
            u = cx.u
            b0, b1 = cx.b
            B1 = "ps%d" % b1
            B0 = "ps%d" % b0
            par = c % 2
            Y = Ys[par]
            tsl = slice(t * 128, (t + 1) * 128)
            DMA("dma_start", ARGS(out=cx.xt[:], in_=x_tile_src(l, c, t)), r=[("xd", l, c, t)], w=[("xt", u)], slot=("xt", u))
            yield
            yield from pad(PADP)
            for kc in range(8):
                T("transpose", ARGS(out=psb(b0)[:, kc * 128:(kc + 1) * 128],
                                    in_=Y[:, t, kc * 128:(kc + 1) * 128], identity=identb[:]),
                  r=[("Y", par, t), "identb"], w=[B0])
                if kc % 4 == 3:
                    yield
            for hf in range(2):
                V("tensor_copy", ARGS(out=T8[:, 4 * hf:4 * hf + 4, tsl],
                                      in_=psb(b0)[:, 512 * hf:512 * hf + 512].rearrange("p (k n) -> p k n", k=4)),
                  r=[B0], w=[("T8", t)])
                yield
            for n in range(2):
                yield from pad(PADP)
                for kc in range(8):
                    T("matmul", ARGS(ps[b1][:], lhsT=T8[:, kc, tsl], rhs=Wout[:, kc, n * 512:(n + 1) * 512],
                                     start=(kc == 0), stop=(kc == 7)), r=[("T8", t), "WKO"], w=[B1])
                yield
                V("tensor_tensor", ARGS(out=cx.xt[:, n * 512:(n + 1) * 512], in0=ps[b1][:],
                                        in1=cx.xt[:, n * 512:(n + 1) * 512], op=ALU.add),
                  r=[B1, ("xt", u)], w=[("xt", u)])
                yield
            DMA("dma_start", ARGS(out=xd[l + 1][c].rearrange("(p t) f -> p t f", t=4)[:, t, :], in_=cx.xt[:]),
                r=[("xt", u)], w=[("xd", l + 1, c, t)], slot=("xst", u))
            yield

        kvctr = [0, 0]
        dqctr = [0]
        BG_STEPS = 4
        PADA = 5
        PADP = 2
        PADC = 45
        PADX = 12
        PADW = 60

        def attention(l, c, bg=None):
            par = c % 2
            Y, QT, MQT, DQc = Ys[par], QTs[par], MQTs[par], DQcs[par]
            YK = [("Y", par, t) for t in range(4)]
            kmg = [gath[("km", l, b)].rearrange("(j h d) n -> d j h n", j=4, h=8) for b in range(4)]
            kdg = [gath[("kd", l, b)].rearrange("(j g r) n -> r j g n", j=4, g=2) for b in range(4)]
            vmg = [gath[("vm", l, b)].rearrange("(j h p) n -> p j h n", j=4, h=8) for b in range(4)]
            vdg = [gath[("vd", l, b)].rearrange("(j h p) n -> p j h n", j=4, h=4) for b in range(4)]
            if True:
                units = []
                for h in range(4):
                    units.append(("mem", h))
                for h in range(8):
                    units.append(("mla", h))
                for m in range(8):
                    units.append(("dif", m))
                bidc = [0]
                tiles = []
                for ui, (kind, h) in enumerate(units):
                    if kind == "mem":
                        tl = []
                        for mt in range(2):
                            tl.append(dict(kap=memKT[:, h, mt * 128:(mt + 1) * 128], vap=memV[:, mt, h, :],
                                           mask=None, kr="memKT", vr="memV", load=None))
                        qap, qk_, scale = MQT[:, h, :], ("MQT", par), 1.0
                        padfn = None
                    else:
                        d = 96 if kind == "mla" else 128
                        kg = kmg if kind == "mla" else kdg
                        vg = vmg if kind == "mla" else vdg
                        vh = h if kind == "mla" else h // 2
                        padfn = None
                        if kind == "mla":
                            qap, qk_ = QT[:, h, :], ("QT", par)
                        else:
                            pk = dqctr[0] % 2
                            dqctr[0] += 1
                            qap, qk_ = DQpad[pk][:], ("DQpad", pk)

                            def padfn(pk=pk, h=h):
                                V("tensor_scalar", ARGS(out=DQpad[pk][:], in0=DQc[:, h // 4, :],
                                                        scalar1=rowmask[:, h % 4:h % 4 + 1], scalar2=None, op0=ALU.mult),
                                  r=[("DQc", par), "rowmask"], w=[("DQpad", pk)])
                        scale = 1.0
                        tl = []
                        kn = "km" if kind == "mla" else "kd"
                        vn = "vm" if kind == "mla" else "vd"
                        hk = h if kind == "mla" else h // 4
                        for b in range(c + 1):
                            if b == c:
                                so = kvctr[1] % 2
                                kvctr[1] += 1
                                if kind == "mla":
                                    ksrc = shard[("km", l, c)].rearrange("(h d) n -> d h n", h=8)[:, hk, :]
                                else:
                                    ksrc = shard[("kd", l, c)].rearrange("(g r) n -> r g n", g=2)[:, hk, :]
                                vsrc = shard[(vn, l, c)].rearrange("(h p) n -> p h n", p=128)[:, vh, :]

                                def load_own(so=so, d=d, ksrc=ksrc, vsrc=vsrc, kn=kn, vn=vn):
                                    DMA("dma_start", ARGS(out=Kown[so][0:d, :], in_=ksrc),
                                        r=[("kvs", kn, l, c)], w=[("Kown", so)], slot=("Kown", so))
                                    DMA("dma_start", ARGS(out=Vown[so][:], in_=vsrc),
                                        r=[("kvs", vn, l, c)], w=[("Vown", so)], slot=("Vown", so))
                                bidc[0] += 1
                                for tk in range(4):
                                    tl.append(dict(kap=Kown[so][0:d, tk * 128:(tk + 1) * 128],
                                                   vap=Vown[so][:, tk * 65:(tk + 1) * 65], mask=tk,
                                                   kr=("Kown", so), vr=("Vown", so), bid=bidc[0], slotkey=("own", so),
                                                   load=load_own if tk == 0 else None))
                            s = kvctr[0] % NKV
                            kvctr[0] += 1

                            def load(s=s, b=b, d=d, kg=kg, vg=vg, vh=vh, hk=hk, kn=kn, vn=vn, diag=(b == c)):
                                DMA("dma_start", ARGS(out=Kbuf[s][0:d, :].rearrange("d (j n) -> d j n", j=4),
                                                      in_=kg[b][:, :, hk, :]),
                                    r=[("kvg", kn, l, b)], w=[("Kbuf", s)], slot=("Kbuf", s))
                                DMA("dma_start", ARGS(out=Vbuf[s][:], in_=vg[b][:, :, vh, :]),
                                    r=[("kvg", vn, l, b)], w=[("Vbuf", s)], slot=("Vbuf", s))

                            def flagfix(s=s):
                                V("tensor_tensor", ARGS(out=Vbuf[s][:],
                                                        in0=flg[:, c * 4:(c + 1) * 4].unsqueeze(2).to_broadcast([128, 4, 260]),
                                                        in1=Vbuf[s][:], op=ALU.mult),
                                  r=["flg", ("Vbuf", s)], w=[("Vbuf", s)])
                            bidc[0] += 1
                            for j in range(4):
                                for tk in range(4):
                                    tl.append(dict(kap=Kbuf[s][0:d, j * 512 + tk * 128: j * 512 + (tk + 1) * 128],
                                                   vap=Vbuf[s][:, j, tk * 65:(tk + 1) * 65], mask=None,
                                                   kr=("Kbuf", s), vr=("Vbuf", s), bid=bidc[0], slotkey=("kv", s),
                                                   fix=flagfix if (b == c and j == 0 and tk == 0) else None,
                                                   load=load if (j == 0 and tk == 0) else None))
                    for i, td in enumerate(tl):
                        td.update(ui=ui, first=(i == 0), last=(i == len(tl) - 1), qap=qap, qk=qk_, scale=scale,
                                  pad=padfn if i == 0 else None)
                        tiles.append(td)

                def post_unit(ui):
                    kind, h = units[ui]
                    acc = 4
                    o = 0
                    V("tensor_copy", ARGS(out=OT[o][:], in_=ps[acc][0:65, :]), r=["ps%d" % acc], w=[("OT", o)])

                    def pe_part():
                        for t in range(4):
                            T("transpose", ARGS(out=ps[5][:, t * 65:(t + 1) * 65], in_=OT[o][0:65, t * 128:(t + 1) * 128],
                                                         identity=identf[0:65, 0:65]), r=[("OT", o), "identf"], w=["ps5"])
                        o3 = ps[5][:, 0:260].rearrange("p (t d) -> p t d", t=4)
                        ri = ui % 2
                        V("reciprocal", ARGS(out=rec[ri][:], in_=o3[:, :, 64]), r=["ps5"], w=[("rec", ri)])
                        if kind in ("mla", "mem"):
                            col = h * 64 if kind == "mla" else 768 + h * 64
                            V("tensor_tensor", ARGS(out=Dh[0][:], in0=rec[ri][:].unsqueeze(2).to_broadcast([128, 4, 64]),
                                                        in1=o3[:, :, 0:64], op=ALU.mult), r=[("rec", ri), "ps5"], w=["Dh0"])
                            V("tensor_tensor", ARGS(out=Y[:, :, col:col + 64], in0=Dh[0][:], in1=Y[:, :, col:col + 64],
                                                        op=ALU.mult), r=["Dh0"] + YK, w=YK)
                        else:
                            hh, mj = h // 2, h % 2
                            col = 512 + hh * 64
                            if mj == 0:
                                V("tensor_tensor", ARGS(out=Dh[1][:], in0=rec[ri][:].unsqueeze(2).to_broadcast([128, 4, 64]),
                                                            in1=o3[:, :, 0:64], op=ALU.mult), r=[("rec", ri), "ps5"], w=["Dh1"])
                            else:
                                V("tensor_scalar", ARGS(out=rec[ri][:], in0=rec[ri][:], scalar1=nlam[:, 0:1], scalar2=None,
                                                            op0=ALU.mult), r=[("rec", ri), "nlam"], w=[("rec", ri)])
                                V("tensor_tensor", ARGS(out=Dh[0][:], in0=rec[ri][:].unsqueeze(2).to_broadcast([128, 4, 64]),
                                                            in1=o3[:, :, 0:64], op=ALU.mult), r=[("rec", ri), "ps5"], w=["Dh0"])
                                V("tensor_tensor", ARGS(out=Dh[1][:], in0=Dh[1][:], in1=Dh[0][:], op=ALU.add),
                                  r=["Dh0", "Dh1"], w=["Dh1"])
                                V("tensor_tensor", ARGS(out=Dh[0][:], in0=Dh[1][:], in1=Dh[1][:], op=ALU.mult),
                                  r=["Dh1"], w=["Dh0"])
                                V("tensor_reduce", ARGS(out=cxp.ss[0][:, 0:4], in_=Dh[0][:], axis=AX.X, op=ALU.add),
                                  r=["Dh0"], w=[("ss", "p", 0)])
                                run(rstd_of(cxp, cxp.ss[0][:, 0:4], 4, 64.0, 0, ("ss", "p", 0)))
                                rr = cxp.rs[0][:, 0:4]
                                V("tensor_tensor", ARGS(out=Dh[1][:], in0=rr.unsqueeze(2).to_broadcast([128, 4, 64]),
                                                                   in1=Dh[1][:], op=ALU.mult), r=[("rs", "p", 0), "Dh1"], w=["Dh1"])
                                V("tensor_tensor", ARGS(out=Dh[1][:], in0=Dh[1][:],
                                                            in1=sgs[:].unsqueeze(1).to_broadcast([128, 4, 64]), op=ALU.mult),
                                  r=["Dh1", "sgs"], w=["Dh1"])
                                V("tensor_tensor", ARGS(out=Y[:, :, col:col + 64], in0=Dh[1][:], in1=Y[:, :, col:col + 64],
                                                            op=ALU.mult), r=["Dh1"] + YK, w=YK)
                    return pe_part

                nt = len(tiles)
                pending = []
                assert nt % 2 == 0
                npair = nt // 2
                blk_last = {}
                for i, td in enumerate(tiles):
                    if td.get("bid") is not None:
                        blk_last[td["bid"]] = i
                sched = {}
                prev_on_slot = {}
                for i, td in enumerate(tiles):
                    if td["load"] is not None:
                        pb = prev_on_slot.get(td["slotkey"])
                        rp = -1 if pb is None else blk_last[pb] // 2
                        assert rp < i // 2 - 2
                        sched.setdefault(rp, []).append(i)
                        prev_on_slot[td["slotkey"]] = td["bid"]
                issued = set()

                def issue(i0):
                    if i0 not in issued:
                        issued.add(i0)
                        tiles[i0]["load"]()

                def do_qk(i):
                    td = tiles[i]
                    if td["load"] is not None:
                        issue(i)
                    if td.get("fix") is not None:
                        td["fix"]()
                    if td.get("pad") is not None:
                        td["pad"]()
                    pr = (i // 2) % 2
                    bk = 2 * pr + (i % 2)
                    T("matmul", ARGS(ps[bk][:], lhsT=td["kap"], rhs=td["qap"], start=True, stop=True),
                      r=[td["kr"], td["qk"]], w=["ps%d" % bk])

                def do_exp(p):
                    pr = p % 2
                    pt = p % NPT
                    A("activation", ARGS(out=PT[pt][:], in_=psall[:, 2 * pr:2 * pr + 2, :], func=AF.Exp),
                      r=["ps%d" % (2 * pr), "ps%d" % (2 * pr + 1)], w=[("PT", pt)])
                    t0, t1 = tiles[2 * p], tiles[2 * p + 1]
                    if t0["mask"] is not None:
                        assert t1["mask"] == t0["mask"] + 1
                        P("affine_select", ARGS(out=PT[pt][:], in_=PT[pt][:], pattern=[[-1, 2], [1, 4], [4, 128]],
                                                compare_op=ALU.is_ge, fill=0.0, base=-t0["mask"], channel_multiplier=-4),
                          r=[("PT", pt)], w=[("PT", pt)])

                def do_pv(p):
                    pt = p % NPT
                    for k in range(2):
                        td = tiles[2 * p + k]
                        acc = 4
                        T("matmul", ARGS(ps[acc][0:65, :], lhsT=td["vap"], rhs=PT[pt][:, k, :], start=td["first"], stop=td["last"]),
                          r=[td["vr"], ("PT", pt)], w=["ps%d" % acc])
                        if td["last"]:
                            fn = post_unit(td["ui"])
                            if units[td["ui"]][0] == "mem":
                                fn()
                            else:
                                pending.append([2, fn])

                for i0 in sched.get(-1, []):
                    issue(i0)
                for i in range(min(4, nt)):
                    do_qk(i)
                for p in range(npair):
                    do_exp(p)
                    if p + 2 < npair:
                        do_qk(2 * p + 4)
                        do_qk(2 * p + 5)
                    do_pv(p)
                    for i0 in sched.get(p, []):
                        issue(i0)
                    if bg is not None:
                        for _ in range(BG_STEPS):
                            next(bg, None)
                    for pnd in list(pending):
                        pnd[0] -= 1
                        if pnd[0] <= 0:
                            pnd[1]()
                            pending.remove(pnd)
                for pnd in pending:
                    pnd[1]()
                pending.clear()

            if bg is not None:
                run(bg)

        assert fused, "only the fused single-launch program is built"

        def pair2(gen_fn, l, c):
            interleave([gen_fn(l, c, 0, cxs[0]), gen_fn(l, c, 1, cxs[1])])
            interleave([gen_fn(l, c, 2, cxs[0]), gen_fn(l, c, 3, cxs[1])])

        def A_bg(l, c):
            if wko[0] != ("K", l):
                ensure_wko("K", l)
                for _ in range(PADW):
                    yield
            for t in range(4):
                yield from A_tile(l, c, t, cxs[1])
            kv_store(l, c)
            yield

        def cc_bg(l, cs):
            for c in cs:
                for nm in ("km", "kd", "vm", "vd"):
                    kv_cc(l, c, nm)
                    for _ in range(PADC):
                        yield

        def O_bg(l, c):
            if wko[0] != ("O", l):
                ensure_wko("O", l)
                for _ in range(PADW):
                    yield
            for t in range(4):
                yield from O_tile(l, c, t, cxs[1])

        def Q_bg(l, c, reload=False):
            if reload:
                load_Q(l)
                for _ in range(PADW):
                    yield
            for t in range(4):
                yield from Q_tile(l, c, t, cxs[1])

        def M_bg(l):
            load_M(l)
            ensure_wko("M", l)
            for _ in range(PADW):
                yield
            for mt in range(2):
                yield from M_tile(l, mt, cxs[1])

        def chain(gens):
            for g in gens:
                yield from g

        for l_ in range(2):
            DMA("dma_start", ARGS(out=Gn[l_][:], in_=W["norm_g"][l_].partition_broadcast(128)),
                w=["g_norm%d" % l_], slot="g_norm%d" % l_)
        load_Kg(0)
        ensure_wko("K", 0)
        for c in range(4):
            pair2(A_tile, 0, c)
            kv_store(0, c)
        for l in range(2):
            if l == 0 or not BG_ON:
                load_M(l)
                mem_kv(l)
            load_P(l)
            if l == 0:
                load_Kg(1)
                load_Q(0)
                pair2(Q_tile, 0, 0)
            for c in range(4):
                parts = []
                if c >= 1:
                    parts.append(O_bg(l, c - 1))
                if l == 0 and c == 2:
                    parts.append(A_bg(1, 0))
                if l == 0 and c == 3:
                    parts.append(A_bg(1, 1))
                    parts.append(A_bg(1, 2))
                if l == 1 and c == 2:
                    parts.append(A_bg(1, 3))
                if c <= 2:
                    parts.append(Q_bg(l, c + 1))
                elif l == 0:
                    parts.append(Q_bg(1, 0, reload=True))
                    parts.append(M_bg(1))
                attention(l, c, chain(parts) if BG_ON else None)
                if not BG_ON:
                    run(chain(parts))
            ensure_wko("O", l)
            pair2(O_tile, l, 3)
        finals = [("dma", ("xst", 0)), ("dma", ("xst", 1))] + [("dma", ("st", k)) for k in range(4)]
        print("[kernel] sbuf bytes remaining:", nc.sbuf_bytes_remaining)
        R.emit(final_waits=finals)
        build.last_counts = dict(R.counts)
    return nc


_NC_CACHE = {}


def _get_nc(mode):
    if mode not in _NC_CACHE:
        _NC_CACHE[mode] = build(mode)
    return _NC_CACHE[mode]


def _flag_table(j):
    t = np.zeros(16, np.float32)
    for lc in range(4):
        for jj in range(4):
            t[lc * 4 + jj] = 1.0 if gchunk(lc, jj) < gchunk(lc, j) else 0.0
    return t


def _common_inputs(inputs, b, j):
    d = {}
    for k in ["norm_g", "w_in", "mla_q_norm_g", "mla_kv_norm_g", "w_uq", "w_ukv", "mla_qn_g", "mla_kn_g",
              "diff_qn_g", "diff_kn_g", "diff_subln_g", "mem_norm_g", "w_mem_kv", "mem_qn_g", "mem_kn_g", "w_out"]:
        d[k] = np.ascontiguousarray(inputs[k], dtype=np.float32)
    d["diff_lambda"] = np.ascontiguousarray(inputs["diff_lambda"], dtype=np.float32).reshape(2, 128)
    pos = np.asarray(inputs["positions"])[b].astype(np.int32)
    d["pos"] = np.stack([pos[gchunk(lc, j) * 512:(gchunk(lc, j) + 1) * 512] for lc in range(4)])
    d["flg"] = _flag_table(j)
    d["mem"] = np.ascontiguousarray(inputs["mem"][b], dtype=np.float32)
    return d


def _chunks(arr_b, j):
    return np.stack([arr_b[gchunk(lc, j) * 512:(gchunk(lc, j) + 1) * 512] for lc in range(4)])


FUSED = True
BG_ON = True


def kernel(**inputs):
    x = np.asarray(inputs["x"], dtype=np.float32)
    cores = [(b, j) for b in range(2) for j in range(4)]
    ids = list(range(8))
    common = [_common_inputs(inputs, b, j) for (b, j) in cores]
    xc = [_chunks(x[b], j) for (b, j) in cores]

    def scatter(res):
        out = np.empty_like(x)
        for ci, (b, j) in enumerate(cores):
            o = np.asarray(res[ci]["out"], dtype=np.float32)
            for lc in range(4):
                g = gchunk(lc, j)
                out[b, g * 512:(g + 1) * 512] = o[lc]
        return out

    if FUSED:
        inf = [dict(common[i], xc=xc[i]) for i in ids]
        rf = run_bass_kernel_spmd(_get_nc("F"), inf, core_ids=ids).results
        return scatter(rf)

    def gather(res, l):
        g = []
        for ci, (b, j) in enumerate(cores):
            d = {}
            for nm in ("km", "kd", "vm", "vd"):
                for c in range(4):
                    d["%sg%d_%d" % (nm, l, c)] = np.concatenate(
                        [res[b * 4 + jj]["%s%d_%d" % (nm, l, c)] for jj in range(4)], axis=0)
            g.append(d)
        return g

    in1 = [dict(common[i], xc=xc[i]) for i in ids]
    r1 = run_bass_kernel_spmd(_get_nc("A0"), in1, core_ids=ids).results
    g0 = gather(r1, 0)
    in2 = [dict(common[i], xc=xc[i], **g0[i]) for i in ids]
    r2 = run_bass_kernel_spmd(_get_nc("B0A1"), in2, core_ids=ids).results
    g1 = gather(r2, 1)
    in3 = [dict(common[i], x1=r2[i]["x1"], **g1[i]) for i in ids]
    r3 = run_bass_kernel_spmd(_get_nc("B1"), in3, core_ids=ids).results
    return scatter(r3)
```

```python
import math
import contextlib
import numpy as np
import ml_dtypes
import concourse.bass as bass
import concourse.mybir as mybir
from concourse.bass_utils import run_bass_kernel_spmd

F32 = mybir.dt.float32
BF16 = mybir.dt.bfloat16
I32 = mybir.dt.int32
ALU = mybir.AluOpType
AF = mybir.ActivationFunctionType
AX = mybir.AxisListType

ENGS = ("pe", "act", "dve", "pool", "sp")
EPS = 1e-6
DEPTH = 2


class Op:
    __slots__ = ("eng", "fn", "deps", "signal", "sigval", "semkey", "is_dma", "idx", "cc")

    def __init__(self, eng, fn, is_dma, semkey):
        self.cc = False
        self.eng = eng
        self.fn = fn
        self.deps = {}
        self.signal = False
        self.sigval = None
        self.semkey = semkey
        self.is_dma = is_dma
        self.idx = None


class Rec:
    def __init__(self, nc):
        self.nc = nc
        self.ops = {e: [] for e in ENGS}
        self.state = {}
        self.nops = 0
        self.last_dma = {}

    def _dep(self, op, prod, raw, war=False):
        if prod is op:
            return
        if prod.semkey == op.semkey and not prod.is_dma:
            if not (raw or war) or op.eng == "pe":
                return
        cur = op.deps.get(prod.semkey)
        if cur is None or cur.idx < prod.idx:
            op.deps[prod.semkey] = prod
        prod.signal = True

    def add(self, eng, name, args, reads=(), writes=(), is_dma=False, slot=None, cc=False):
        semkey = ("dma", slot) if is_dma else eng
        op = Op(eng, (name, args), is_dma, semkey)
        op.cc = cc
        op.idx = self.nops
        self.nops += 1
        if is_dma:
            op.signal = True
            prev = self.last_dma.get(semkey)
            if prev is not None:
                self._dep(op, prev, True)
            self.last_dma[semkey] = op
        reads = list(reads)
        writes = list(writes)
        ps_reads = [k for k in reads if isinstance(k, str) and k.startswith("ps")]
        if ps_reads:
            reads = [k for k in reads if k not in ps_reads]
            for k in ps_reads:
                st = self.state.setdefault(k, [{}, {}])
                for w in st[0].values():
                    self._dep(op, w, True)
            writes = writes + ps_reads
        for k in reads:
            st = self.state.setdefault(k, [{}, {}])
            for w in st[0].values():
                self._dep(op, w, True)
        for k in writes:
            st = self.state.setdefault(k, [{}, {}])
            for r in st[1].values():
                self._dep(op, r, False, war=True)
            for w in st[0].values():
                self._dep(op, w, False)
        for k in reads:
            self.state[k][1][semkey] = op
        for k in writes:
            st = self.state[k]
            if st[1]:
                st[0] = {}
                st[1] = {}
            st[0][semkey] = op
        self.ops[eng].append(op)
        return op

    def emit(self, final_waits=()):
        nc = self.nc
        counts = {}
        for e in ENGS:
            for op in self.ops[e]:
                if op.signal:
                    inc = 16 if (op.is_dma and not op.cc) else 1
                    counts[op.semkey] = counts.get(op.semkey, 0) + inc
                    op.sigval = counts[op.semkey]
        self.counts = counts
        print("[kernel] semaphores needed:", len(counts), "max count:", max(counts.values()))
        with contextlib.ExitStack() as es:
            sems = {}
            for i, k in enumerate(counts.keys()):
                sems[k] = es.enter_context(nc.semaphore("s%d" % i))
            block = es.enter_context(nc.Block())
            engmap = {"pe": block.tensor, "act": block.scalar, "dve": block.vector,
                      "pool": block.gpsimd, "sp": block.sync}

            def make(e):
                def body(eng):
                    known = {}
                    for op in self.ops[e]:
                        for sk, prod in op.deps.items():
                            v = prod.sigval
                            if known.get(sk, 0) < v:
                                eng.wait_ge(sems[sk], v)
                                known[sk] = v
                        ins = getattr(eng, op.fn[0])(*op.fn[1][0], **op.fn[1][1])
                        if op.signal:
                            if op.cc:
                                ins.then_inc(sems[op.semkey])
                            else:
                                ins.then_inc(sems[op.semkey], 16 if op.is_dma else 1)
                    if e == "sp":
                        for sk in final_waits:
                            if sk in counts:
                                eng.wait_ge(sems[sk], counts[sk])
                return body

            for e in ENGS:
                if self.ops[e] or e == "sp":
                    engmap[e](make(e))


def ARGS(*a, **k):
    return (a, k)


def gchunk(lc, j):
    return [j, 7 - j, 8 + j, 15 - j][lc]


WIN_K_COLS = [(256, 416), (672, 1184)]
WIN_Q_COLS = [(0, 256), (416, 672), (1184, 1440), (1440, 2464)]

KVS = {"km": [768, 512], "kd": [256, 512], "vm": [1024, 260], "vd": [512, 260]}
GROUPS = [[0, 1, 2, 3], [4, 5, 6, 7]]

PHASES = {"A0": [("A", 0)], "B0A1": [("B", 0), ("A", 1)], "B1": [("B", 1)],
          "F": [("A", 0), ("B", 0), ("A", 1), ("B", 1)]}


def build(mode):
    phases = PHASES[mode]
    fused = mode == "F"
    nc = bass.Bass("TRN2", target_bir_lowering=False)
    IN, OUT, INT = "ExternalInput", "ExternalOutput", "Internal"

    def dram(name, shape, dt, kind):
        if kind is None:
            return nc.dram_tensor(name, list(shape), dt).ap()
        return nc.dram_tensor(name, list(shape), dt, kind=kind).ap()

    prodB = {l for p, l in phases if p == "B"}
    prodA = {l for p, l in phases if p == "A"}
    xd = {}
    xd[0] = dram("xc", [4, 512, 1024], F32, IN) if (0 in prodA or 0 in prodB) else None
    if 0 in prodB:
        xd[1] = dram("x1", [4, 512, 1024], F32, None if fused else OUT)
    elif 1 in prodA or 1 in prodB:
        xd[1] = dram("x1", [4, 512, 1024], F32, IN)
    if 1 in prodB:
        xd[2] = dram("out", [4, 512, 1024], F32, OUT)
    pos_d = dram("pos", [4, 512], I32, IN)
    flg_d = dram("flg", [16], F32, IN)
    mem_d = dram("mem", [256, 1024], F32, IN)
    W = {}
    for name, shape in [("norm_g", [2, 1024]), ("w_in", [2, 1024, 2464]), ("mla_q_norm_g", [2, 256]),
                        ("mla_kv_norm_g", [2, 128]), ("w_uq", [2, 256, 768]), ("w_ukv", [2, 128, 1024]),
                        ("mla_qn_g", [2, 96]), ("mla_kn_g", [2, 96]), ("diff_qn_g", [2, 32]),
                        ("diff_kn_g", [2, 32]), ("diff_lambda", [2, 128]), ("diff_subln_g", [2, 64]),
                        ("mem_norm_g", [2, 1024]), ("w_mem_kv", [2, 1024, 512]), ("mem_qn_g", [2, 64]),
                        ("mem_kn_g", [2, 64]), ("w_out", [2, 1024, 1024])]:
        W[name] = dram(name, shape, F32, IN)
    shard = {}
    gath = {}
    for l in range(DEPTH):
        for nm, shp in KVS.items():
            for c in range(4):
                if l in prodA:
                    shard[(nm, l, c)] = dram("%s%d_%d" % (nm, l, c), shp, BF16, None if fused else OUT)
                if l in prodB:
                    gath[(nm, l, c)] = dram("%sg%d_%d" % (nm, l, c), [4 * shp[0], shp[1]], BF16, None if fused else IN)

    es = contextlib.ExitStack()
    with es:
        def sb(name, shape, dt):
            return es.enter_context(nc.sbuf_tensor(name, list(shape), dt))

        WinQ = sb("WinQ", [128, 8, 1792], BF16)
        Wuq = sb("Wuq", [128, 2, 768], BF16)
        WKO = sb("WKO", [128, 8192], BF16)
        Wout = WKO[:].rearrange("p (c n) -> p c n", c=8)
        WinK = WKO[:, 0:5376].rearrange("p (c n) -> p c n", c=8)
        Wukv = WKO[:, 5376:6400]
        Wmem = WKO[:, 0:4096].rearrange("p (c n) -> p c n", c=8)
        G = {}
        Gn = [sb("g_norm%d" % l_, [128, 1024], F32) for l_ in range(2)]
        for nm, n in [("mem_norm_g", 1024), ("mla_q_norm_g", 256), ("mla_kv_norm_g", 128),
                      ("mla_qn_g", 96), ("mla_kn_g", 96), ("diff_qn_g", 32), ("diff_kn_g", 32),
                      ("diff_lambda", 128), ("diff_subln_g", 64), ("mem_qn_g", 64), ("mem_kn_g", 64)]:
            G[nm] = sb("g_" + nm, [128, n], F32)
        identb = sb("identb", [128, 128], BF16)
        identf = sb("identf", [128, 128], F32)
        mhalf = sb("mhalf", [128, 8], F32)
        flg = sb("flg_sb", [128, 16], F32)
        posi = sb("posi", [128, 16], I32)
        posf = sb("posf", [128, 16], F32)
        invf = sb("invf", [128, 16], F32)
        cosT = sb("cosT", [128, 16, 16], F32)
        sinT = sb("sinT", [128, 16, 16], F32)
        lam_s = sb("lam_s", [128, 4], F32)
        nlam = sb("nlam", [128, 1], F32)
        sgs = sb("sgs", [128, 64], F32)
        hb = [sb("hb%d" % i, [128, 1024], BF16) for i in range(2)]
        T8 = sb("T8", [128, 8, 512], BF16)
        Ys = [sb("Y%d" % i, [128, 4, 1024], BF16) for i in range(2)]
        QTs = [sb("QT%d" % i, [96, 8, 512], BF16) for i in range(2)]
        DQcs = [sb("DQc%d" % i, [128, 2, 512], BF16) for i in range(2)]
        DQpad = [sb("DQpad%d" % i, [128, 512], BF16) for i in range(2)]
        KTst = sb("KTst", [96, 8, 512], BF16)
        DKT = sb("DKT", [128, 2, 512], BF16)
        rowmask = sb("rowmask", [128, 4], F32)
        onesf = sb("onesf", [128, 4], F32)
        MQTs = [sb("MQT%d" % i, [64, 4, 512], BF16) for i in range(2)]
        Vst = sb("Vst", [128, 8, 4, 65], BF16)
        VDst = sb("VDst", [128, 4, 4, 65], BF16)
        NKV = 2
        Kbuf = [sb("Kbuf%d" % i, [128, 2048], BF16) for i in range(NKV)]
        Vbuf = [sb("Vbuf%d" % i, [128, 4, 260], BF16) for i in range(NKV)]
        Kown = [sb("Kown%d" % i, [128, 512], BF16) for i in range(2)]
        Vown = [sb("Vown%d" % i, [128, 260], BF16) for i in range(2)]
        NPT = 3
        PT = [sb("PT%d" % i, [128, 2, 512], BF16) for i in range(NPT)]
        OT = [sb("OT%d" % i, [65, 512], F32) for i in range(1)]
        memKT = sb("memKT", [64, 4, 256], BF16)
        memV = sb("memV", [128, 2, 4, 65], BF16)
        class CX:
            pass
        cxs = []
        for u in range(2):
            cx = CX()
            cx.u = u
            cx.sq = sb("sq%d" % u, [128, 1024], F32)
            cx.ss = [sb("ss%d_%d" % (u, i), [128, 8], F32) for i in range(4)]
            cx.rs = [sb("rs%d_%d" % (u, i), [128, 8], F32) for i in range(4)]
            cx.f3 = [sb("f3_%d_%d" % (u, i), [128, 8, 96], F32) for i in range(2)]
            cx.r16 = [sb("r16_%d_%d" % (u, i), [128, 8, 16], F32) for i in range(4)]
            cx.c256 = sb("c256_%d" % u, [128, 256], BF16)
            cx.cT = sb("cT%d" % u, [128, 2, 128], BF16)
            cx.Kb = sb("Kb%d" % u, [128, 8, 96], BF16)
            cx.DKc = sb("DKc%d" % u, [128, 8, 32], BF16)
            cx.krs = sb("krs%d" % u, [128, 32], F32)
            cx.krr = sb("krr%d" % u, [128, 32], F32)
            cx.hb = hb[u]
            cx.xt = sb("xt%d" % u, [128, 1024], F32)
            cx.b = [0, 1] if u == 0 else [6, 7]
            cxs.append(cx)
        cxp = CX()
        cxp.u = "p"
        cxp.ss = [sb("pss", [128, 8], F32)]
        cxp.rs = [sb("prs", [128, 8], F32)]
        Dh = [sb("Dh%d" % i, [128, 4, 64], F32) for i in range(2)]
        rec = [sb("rec%d" % i, [128, 4], F32) for i in range(2)]

        psall = es.enter_context(nc.psum_tensor("psall", [128, 8, 512], F32))
        _sq1 = cxs[1].sq
        ang = _sq1[:, 0:256].rearrange("p (a b) -> p a b", a=16)
        angk = _sq1[:, 256:512].rearrange("p (a b) -> p a b", a=16)
        angi = _sq1[:, 512:768].rearrange("p (a b) -> p a b", a=16).bitcast(I32)

        class _Bank:
            def __init__(self, i):
                self.i = i

            def __getitem__(self, idx):
                return psall[:, self.i, :][idx]
        ps = [_Bank(i) for i in range(8)]
        R = Rec(nc)

        def psb(i):
            return psall[:, i, :].bitcast(BF16)

        def V(name, args, r=(), w=()):
            return R.add("dve", name, args, r, w)

        def A(name, args, r=(), w=()):
            return R.add("act", name, args, r, w)

        def P(name, args, r=(), w=()):
            return R.add("pool", name, args, r, w)

        def T(name, args, r=(), w=()):
            return R.add("pe", name, args, r, w)

        def DMA(name, args, r=(), w=(), slot=None, q="sp"):
            return R.add(q, name, args, r, w, is_dma=True, slot=slot)

        P("memset", ARGS(identb[:], 0.0), w=["identb"])
        P("memset", ARGS(identf[:], 0.0), w=["identf"])
        P("memset", ARGS(mhalf[:], -0.5), w=["mhalf"])
        P("affine_select", ARGS(out=identb[:], in_=identb[:], pattern=[[1, 128]], compare_op=ALU.not_equal,
                                    fill=1.0, base=0, channel_multiplier=-1), r=["identb"], w=["identb"])
        P("affine_select", ARGS(out=identf[:], in_=identf[:], pattern=[[1, 128]], compare_op=ALU.not_equal,
                                    fill=1.0, base=0, channel_multiplier=-1), r=["identf"], w=["identf"])
        P("memset", ARGS(onesf[:], 1.0), w=["onesf"])
        for m in range(4):
            P("affine_select", ARGS(out=rowmask[:, m:m + 1], in_=onesf[:, m:m + 1], pattern=[[0, 1]], compare_op=ALU.is_ge,
                                    fill=0.0, base=-32 * m, channel_multiplier=1), r=["onesf", "rowmask"], w=["rowmask"])
            P("affine_select", ARGS(out=rowmask[:, m:m + 1], in_=rowmask[:, m:m + 1], pattern=[[0, 1]], compare_op=ALU.is_ge,
                                    fill=0.0, base=32 * m + 31, channel_multiplier=-1), r=["rowmask"], w=["rowmask"])
        P("memset", ARGS(Vst[:], 1.0), w=["Vst"])
        P("memset", ARGS(VDst[:], 1.0), w=["VDst"])
        P("memset", ARGS(memV[:], 1.0), w=["memV"])
        DMA("dma_start", ARGS(out=flg[:], in_=flg_d.partition_broadcast(128)), w=["flg"], slot="flg")
        DMA("dma_start", ARGS(out=posi[:].rearrange("p (c t) -> p c t", c=4),
                                  in_=pos_d.rearrange("c (p t) -> p c t", t=4)), w=["posi"], slot="posi")
        V("tensor_copy", ARGS(out=posf[:], in_=posi[:]), r=["posi"], w=["posf"])
        for i in range(16):
            val = float(np.float32(10000.0 ** (-(2.0 * i) / 32.0)))
            P("memset", ARGS(invf[:, i:i + 1], val), w=["invf"])
        V("tensor_tensor", ARGS(out=ang, in0=posf[:].unsqueeze(2).to_broadcast([128, 16, 16]),
                                    in1=invf[:].unsqueeze(1).to_broadcast([128, 16, 16]), op=ALU.mult),
          r=["posf", "invf"], w=[("sq", 1)])
        TWO_PI = 2.0 * math.pi
        c1 = 6.28125
        c2 = float(np.round((TWO_PI - c1) * 2 ** 20) / 2 ** 20)
        c3 = float(TWO_PI - c1 - c2)
        V("tensor_scalar", ARGS(out=angi, in0=ang, scalar1=1.0 / TWO_PI, scalar2=None, op0=ALU.mult),
          r=[("sq", 1)], w=[("sq", 1)])
        V("tensor_copy", ARGS(out=angk, in_=angi), r=[("sq", 1)], w=[("sq", 1)])
        for cc in (c1, c2, c3):
            V("scalar_tensor_tensor", ARGS(out=ang, in0=angk, scalar=-cc, in1=ang,
                                                      op0=ALU.mult, op1=ALU.add), r=[("sq", 1)], w=[("sq", 1)])
        PI_IN = 3.1415925
        V("tensor_scalar", ARGS(out=sinT[:], in0=ang, scalar1=PI_IN, scalar2=-PI_IN, op0=ALU.min, op1=ALU.max),
          r=[("sq", 1)], w=["sinT"])
        V("tensor_scalar", ARGS(out=cosT[:], in0=ang, scalar1=math.pi / 2, scalar2=None, op0=ALU.add),
          r=[("sq", 1)], w=["cosT"])
        V("tensor_scalar", ARGS(out=angk, in0=cosT[:], scalar1=math.pi, scalar2=None, op0=ALU.is_gt),
          r=["cosT"], w=[("sq", 1)])
        V("scalar_tensor_tensor", ARGS(out=cosT[:], in0=angk, scalar=-TWO_PI, in1=cosT[:],
                                           op0=ALU.mult, op1=ALU.add), r=[("sq", 1), "cosT"], w=["cosT"])
        V("tensor_scalar", ARGS(out=cosT[:], in0=cosT[:], scalar1=PI_IN, scalar2=-PI_IN, op0=ALU.min, op1=ALU.max),
          r=["cosT"], w=["cosT"])
        A("activation", ARGS(out=sinT[:], in_=sinT[:], func=AF.Sin), r=["sinT"], w=["sinT"])
        A("activation", ARGS(out=cosT[:], in_=cosT[:], func=AF.Sin), r=["cosT"], w=["cosT"])

        def NG(l):
            return "norm_g"

        def interleave(gens):
            gens = list(gens)
            while gens:
                for g in list(gens):
                    try:
                        next(g)
                    except StopIteration:
                        gens.remove(g)

        def pad(n):
            for _ in range(n):
                yield

        def run(g):
            for _ in g:
                pass

        def rstd_of(cx, ssum_ap, n, Dn, i, rk):
            V("tensor_scalar", ARGS(out=ssum_ap, in0=ssum_ap, scalar1=1.0 / Dn, scalar2=EPS,
                                    op0=ALU.mult, op1=ALU.add), r=[rk], w=[rk])
            yield
            P("tensor_tensor", ARGS(out=cx.rs[i][:, 0:n], in0=ssum_ap, in1=mhalf[:, 0:n], op=ALU.pow),
              r=[rk, "mhalf"], w=[("rs", cx.u, i)])
            yield
            return cx.rs[i][:, 0:n]

        def wl(dst, src, key):
            DMA("dma_start", ARGS(out=dst, in_=src), w=[key], slot=key, q="pool")

        def gl(nm, l):
            DMA("dma_start", ARGS(out=G[nm][:], in_=W[nm][l].partition_broadcast(128)), w=["g_" + nm], slot="g_" + nm)

        wko = [None]

        def ensure_wko(what, l):
            if wko[0] == (what, l):
                return
            wko[0] = (what, l)
            if what == "K":
                win = W["w_in"][l].rearrange("(c p) n -> p c n", p=128)
                off = 0
                for i, (a_, b_) in enumerate(WIN_K_COLS):
                    DMA("dma_start", ARGS(out=WinK[:, :, off:off + (b_ - a_)], in_=win[:, :, a_:b_]), w=["WKO"],
                        slot=("WKO", i), q="pool")
                    off += b_ - a_
                DMA("dma_start", ARGS(out=Wukv, in_=W["w_ukv"][l]), w=["WKO"], slot=("WKO", 2), q="pool")
            elif what == "O":
                DMA("dma_start", ARGS(out=Wout, in_=W["w_out"][l].rearrange("(c p) n -> p c n", p=128)), w=["WKO"],
                    slot=("WKO", 0), q="pool")
            else:
                DMA("dma_start", ARGS(out=Wmem, in_=W["w_mem_kv"][l].rearrange("(c p) n -> p c n", p=128)), w=["WKO"],
                    slot=("WKO", 0), q="pool")

        def load_Kg(l):
            for nm in ("mla_kv_norm_g", "mla_kn_g", "diff_kn_g"):
                gl(nm, l)

        def load_Q(l):
            win = W["w_in"][l].rearrange("(c p) n -> p c n", p=128)
            off = 0
            for i, (a_, b_) in enumerate(WIN_Q_COLS):
                wl(WinQ[:, :, off:off + (b_ - a_)], win[:, :, a_:b_], "WinQ%d" % i)
                off += b_ - a_
            wl(Wuq[:], W["w_uq"][l].rearrange("(c p) n -> p c n", p=128), "Wuq")
            for nm in ("mla_q_norm_g", "mla_qn_g", "diff_qn_g", "mem_qn_g"):
                gl(nm, l)

        def load_M(l):
            for nm in ("mem_norm_g", "mem_kn_g"):
                gl(nm, l)

        def load_P(l):
            for nm in ("diff_lambda", "diff_subln_g"):
                gl(nm, l)
            lam_init = 0.8 - 0.6 * math.exp(-0.3 * l)
            lv = G["diff_lambda"][:].rearrange("p (a b d) -> p a b d", a=2, b=2)
            V("tensor_tensor", ARGS(out=cxs[0].sq[:, 0:64].rearrange("p (a d) -> p a d", a=2), in0=lv[:, :, 0, :],
                                    in1=lv[:, :, 1, :], op=ALU.mult), r=["g_diff_lambda"], w=[("sq", 0)])
            V("tensor_reduce", ARGS(out=lam_s[:, 0:2], in_=cxs[0].sq[:, 0:64].rearrange("p (a d) -> p a d", a=2),
                                    axis=AX.X, op=ALU.add), r=[("sq", 0)], w=["lam_s"])
            A("activation", ARGS(out=lam_s[:, 2:4], in_=lam_s[:, 0:2], func=AF.Exp), r=["lam_s"], w=["lam_s2"])
            V("tensor_tensor", ARGS(out=nlam[:], in0=lam_s[:, 3:4], in1=lam_s[:, 2:3], op=ALU.subtract),
              r=["lam_s2"], w=["nlam"])
            V("tensor_scalar", ARGS(out=nlam[:], in0=nlam[:], scalar1=-lam_init, scalar2=None, op0=ALU.add),
              r=["nlam"], w=["nlam"])
            V("tensor_scalar", ARGS(out=sgs[:], in0=G["diff_subln_g"][:], scalar1=1.0 - lam_init, scalar2=None,
                                    op0=ALU.mult), r=["g_diff_subln_g"], w=["sgs"])

        WK = ["WKO", "WKO"]
        WQ = ["WinQ0", "WinQ1", "WinQ2", "WinQ3"]

        WK = ["WKO", "WKO"]
        WQ = ["WinQ0", "WinQ1", "WinQ2", "WinQ3"]

        def norm_transpose(cx, gap, gkey, src_ap, srck, dstT, dstk, col0):
            u, b0 = cx.u, cx.b[0]
            yield from pad(PADA)
            A("activation", ARGS(out=cx.sq[:], in_=src_ap, func=AF.Square, accum_out=cx.ss[0][:, 0:1]),
              r=[srck], w=[("sq", u), ("ss", u, 0)])
            yield
            rr = yield from rstd_of(cx, cx.ss[0][:, 0:1], 1, 1024.0, 0, ("ss", u, 0))
            for hf in range(2):
                hs = slice(hf * 512, (hf + 1) * 512)
                V("scalar_tensor_tensor", ARGS(out=cx.hb[:, hs], in0=src_ap[:, hs], scalar=rr, in1=gap[:, hs],
                                               op0=ALU.mult, op1=ALU.mult),
                  r=[srck, ("rs", u, 0), gkey], w=[("hb", u)])
                yield
            yield from pad(PADP)
            for kc in range(8):
                T("transpose", ARGS(out=psb(b0)[:, kc * 128:(kc + 1) * 128],
                                    in_=cx.hb[:, kc * 128:(kc + 1) * 128], identity=identb[:]),
                  r=[("hb", u), "identb"], w=["ps%d" % b0])
                if kc % 4 == 3:
                    yield
            for hf in range(2):
                V("tensor_copy", ARGS(out=dstT[:, 4 * hf:4 * hf + 4, col0:col0 + 128],
                                      in_=psb(b0)[:, 512 * hf:512 * hf + 512].rearrange("p (k n) -> p k n", k=4)),
                  r=["ps%d" % b0], w=[dstk])
                yield

        def rope(cx, src3, srck, dst3, dstk, lo, ct, nh):
            u = cx.u
            x1 = src3[:, :, lo:lo + 16]
            x2 = src3[:, :, lo + 16:lo + 32]
            cs = cosT[:, ct, :].unsqueeze(1).to_broadcast([128, nh, 16])
            sn = sinT[:, ct, :].unsqueeze(1).to_broadcast([128, nh, 16])
            a, b, c_, d = [cx.r16[k][:, 0:nh, :] for k in range(4)]
            V("tensor_tensor", ARGS(out=a, in0=x1, in1=cs, op=ALU.mult), r=[srck, "cosT"], w=[("r16", u, 0)])
            yield
            V("tensor_tensor", ARGS(out=b, in0=x2, in1=sn, op=ALU.mult), r=[srck, "sinT"], w=[("r16", u, 1)])
            yield
            V("tensor_tensor", ARGS(out=c_, in0=x2, in1=cs, op=ALU.mult), r=[srck, "cosT"], w=[("r16", u, 2)])
            yield
            V("tensor_tensor", ARGS(out=d, in0=x1, in1=sn, op=ALU.mult), r=[srck, "sinT"], w=[("r16", u, 3)])
            yield
            V("tensor_tensor", ARGS(out=dst3[:, :, lo:lo + 16], in0=a, in1=b, op=ALU.subtract),
              r=[("r16", u, 0), ("r16", u, 1)], w=[dstk])
            yield
            V("tensor_tensor", ARGS(out=dst3[:, :, lo + 16:lo + 32], in0=c_, in1=d, op=ALU.add),
              r=[("r16", u, 2), ("r16", u, 3)], w=[dstk])
            yield

        def head_norm(cx, src3, srck, nh, Dh_, gname, si, dst3, dstk, scale=None):
            u = cx.u
            sqv = cx.sq[:, 0:nh * Dh_].rearrange("p (h d) -> p h d", h=nh)
            yield from pad(PADA)
            A("activation", ARGS(out=sqv, in_=src3, func=AF.Square), r=[srck], w=[("sq", u)])
            yield
            V("tensor_reduce", ARGS(out=cx.ss[si][:, 0:nh], in_=sqv, axis=AX.X, op=ALU.add),
              r=[("sq", u)], w=[("ss", u, si)])
            yield
            rr = yield from rstd_of(cx, cx.ss[si][:, 0:nh], nh, float(Dh_), si, ("ss", u, si))
            if scale is not None:
                V("tensor_scalar", ARGS(out=rr, in0=rr, scalar1=float(scale), scalar2=None, op0=ALU.mult),
                  r=[("rs", u, si)], w=[("rs", u, si)])
                yield
            V("tensor_tensor", ARGS(out=dst3, in0=src3, in1=G[gname][:, 0:Dh_].unsqueeze(1).to_broadcast([128, nh, Dh_]),
                                    op=ALU.mult), r=[srck, "g_" + gname], w=[dstk])
            yield
            V("tensor_tensor", ARGS(out=dst3, in0=rr.unsqueeze(2).to_broadcast([128, nh, Dh_]), in1=dst3,
                                    op=ALU.mult), r=[("rs", u, si), dstk], w=[dstk])
            yield

        def diff_pack(cx, ct):
            u, b0 = cx.u, cx.b[0]
            yield from rope(cx, cx.f3[1], ("f3", u, 1), cx.f3[1], ("f3", u, 1), 0, ct, 8)
            V("tensor_copy", ARGS(out=cx.DKc[:], in_=cx.f3[1][:, :, 0:32]), r=[("f3", u, 1)], w=[("DKc", u)])
            yield
            yield from pad(PADP)
            for g in range(2):
                T("transpose", ARGS(out=psb(b0)[:, g * 128:(g + 1) * 128],
                                    in_=cx.DKc[:, 4 * g:4 * g + 4, :].rearrange("p m d -> p (m d)"),
                                    identity=identb[:]), r=[("DKc", u), "identb"], w=["ps%d" % b0])
            yield

        def x_tile_src(l, c, t):
            return xd[l][c].rearrange("(p t) f -> p t f", t=4)[:, t, :]

        def A_tile(l, c, t, cx):
            u = cx.u
            b0, b1 = cx.b
            B1 = "ps%d" % b1
            B0 = "ps%d" % b0
            ct = c * 4 + t
            tsl = slice(t * 128, (t + 1) * 128)
            DMA("dma_start", ARGS(out=cx.xt[:], in_=x_tile_src(l, c, t)), r=[("xd", l, c, t)], w=[("xt", u)], slot=("xt", u))
            for _ in range(PADX):
                yield
            yield from norm_transpose(cx, Gn[l][:], "g_norm%d" % l, cx.xt[:], ("xt", u), T8, ("T8", t), t * 128)
            yield from pad(PADP)
            for kc in range(8):
                T("matmul", ARGS(ps[b1][:], lhsT=T8[:, kc, tsl], rhs=WinK[:, kc, 160:672],
                                 start=(kc == 0), stop=(kc == 7)), r=[("T8", t), "WKO"], w=[B1])
            yield
            dk3 = ps[b1][:, 0:256].rearrange("p (m d) -> p m d", m=8)
            yield from head_norm(cx, dk3, B1, 8, 32, "diff_kn_g", 0, cx.f3[1][:, :, 0:32], ("f3", u, 1))
            V("tensor_copy", ARGS(out=VDst[:, :, t, 0:64], in_=ps[b1][:, 256:512].rearrange("p (h d) -> p h d", h=4)),
              r=[B1], w=["VDst"])
            yield
            yield from pad(PADP)
            for kc in range(8):
                T("matmul", ARGS(ps[b1][:, 0:160], lhsT=T8[:, kc, tsl], rhs=WinK[:, kc, 0:160],
                                 start=(kc == 0), stop=(kc == 7)), r=[("T8", t), "WKO"], w=[B1])
            yield
            yield from pad(PADA)
            A("activation", ARGS(out=cx.sq[:, 0:128], in_=ps[b1][:, 0:128], func=AF.Square,
                                 accum_out=cx.ss[2][:, 0:1]), r=[B1], w=[("sq", u), ("ss", u, 2)])
            yield
            rr = yield from rstd_of(cx, cx.ss[2][:, 0:1], 1, 128.0, 2, ("ss", u, 2))
            V("scalar_tensor_tensor", ARGS(out=cx.c256[:, 0:128], in0=ps[b1][:, 0:128], scalar=rr,
                                           in1=G["mla_kv_norm_g"][:], op0=ALU.mult, op1=ALU.mult),
              r=[B1, ("rs", u, 2), "g_mla_kv_norm_g"], w=[("c256", u)])
            yield
            T("transpose", ARGS(out=psb(b0)[:, 0:128], in_=cx.c256[:, 0:128], identity=identb[:]),
              r=[("c256", u), "identb"], w=[B0])
            yield
            V("tensor_copy", ARGS(out=cx.cT[:, 0, :], in_=psb(b0)[:, 0:128]), r=[B0], w=[("cT", u)])
            yield
            yield from pad(PADA)
            A("activation", ARGS(out=cx.krs[:], in_=ps[b1][:, 128:160], func=AF.Square,
                                 accum_out=cx.ss[3][:, 0:1]), r=[B1], w=[("krs", u), ("ss", u, 3)])
            yield
            V("tensor_tensor", ARGS(out=cx.krs[:], in0=ps[b1][:, 128:160], in1=G["mla_kn_g"][:, 64:96], op=ALU.mult),
              r=[B1, "g_mla_kn_g"], w=[("krs", u)])
            yield
            yield from rope(cx, cx.krs[:].unsqueeze(1), ("krs", u), cx.krr[:].unsqueeze(1), ("krr", u), 0, ct, 1)
            for n in range(2):
                T("matmul", ARGS(ps[b1][:], lhsT=cx.cT[:, 0, :], rhs=Wukv[:, n * 512:(n + 1) * 512],
                                 start=True, stop=True), r=[("cT", u), "WKO"], w=[B1])
                yield
                kvv = ps[b1][:].rearrange("p (h d) -> p h d", h=4)
                yield from pad(PADA)
                A("activation", ARGS(out=cx.sq[:, n * 256:(n + 1) * 256].rearrange("p (h d) -> p h d", h=4),
                                     in_=kvv[:, :, 0:64], func=AF.Square), r=[B1], w=[("sq", u)])
                yield
                V("tensor_tensor", ARGS(out=cx.f3[0][:, n * 4:(n + 1) * 4, 0:64], in0=kvv[:, :, 0:64],
                                        in1=G["mla_kn_g"][:, 0:64].unsqueeze(1).to_broadcast([128, 4, 64]), op=ALU.mult),
                  r=[B1, "g_mla_kn_g"], w=[("f3", u, 0)])
                yield
                V("tensor_copy", ARGS(out=Vst[:, n * 4:(n + 1) * 4, t, 0:64], in_=kvv[:, :, 64:128]),
                  r=[B1], w=["Vst"])
                yield
            V("tensor_reduce", ARGS(out=cx.ss[1][:, 0:8], in_=cx.sq[:, 0:512].rearrange("p (h d) -> p h d", h=8),
                                    axis=AX.X, op=ALU.add), r=[("sq", u)], w=[("ss", u, 1)])
            yield
            V("tensor_scalar", ARGS(out=cx.ss[1][:, 0:8], in0=cx.ss[1][:, 0:8], scalar1=cx.ss[3][:, 0:1], scalar2=None,
                                    op0=ALU.add), r=[("ss", u, 1), ("ss", u, 3)], w=[("ss", u, 1)])
            yield
            rk = yield from rstd_of(cx, cx.ss[1][:, 0:8], 8, 96.0, 1, ("ss", u, 1))
            V("tensor_tensor", ARGS(out=cx.Kb[:, :, 0:64], in0=rk.unsqueeze(2).to_broadcast([128, 8, 64]),
                                    in1=cx.f3[0][:, :, 0:64], op=ALU.mult), r=[("rs", u, 1), ("f3", u, 0)], w=[("Kb", u)])
            yield
            V("tensor_tensor", ARGS(out=cx.Kb[:, :, 64:96], in0=rk.unsqueeze(2).to_broadcast([128, 8, 32]),
                                    in1=cx.krr[:].unsqueeze(1).to_broadcast([128, 8, 32]), op=ALU.mult),
              r=[("rs", u, 1), ("krr", u)], w=[("Kb", u)])
            yield
            yield from pad(PADP)
            for h in range(8):
                T("transpose", ARGS(out=psb(b0)[0:96, h * 128:(h + 1) * 128], in_=cx.Kb[:, h, :],
                                    identity=identb[:]), r=[("Kb", u), "identb"], w=[B0])
                if h % 4 == 3:
                    yield
            for hf in range(2):
                V("tensor_copy", ARGS(out=KTst[:, 4 * hf:4 * hf + 4, tsl],
                                      in_=psb(b0)[0:96, 512 * hf:512 * hf + 512].rearrange("p (h n) -> p h n", h=4)),
                  r=[B0], w=["KTst"])
                yield
            yield from diff_pack(cx, ct)
            V("tensor_copy", ARGS(out=DKT[:, :, tsl], in_=psb(b0)[:, 0:256].rearrange("p (g n) -> p g n", g=2)),
              r=[B0], w=["DKT"])
            yield

        def kv_cc(l, c, nm):
            R.add("pool", "collective_compute",
                  ARGS("AllGather", ALU.bypass, replica_groups=GROUPS,
                       ins=[shard[(nm, l, c)].opt()], outs=[gath[(nm, l, c)].opt()]),
                  reads=[("kvs", nm, l, c)], writes=[("kvg", nm, l, c)],
                  is_dma=True, slot=("cc", nm), cc=True)

        def kv_store(l, c, cc_now=True):
            if True:
                DMA("dma_start", ARGS(out=shard[("km", l, c)].rearrange("(h d) n -> d h n", h=8), in_=KTst[:]),
                    r=["KTst"], w=[("kvs", "km", l, c)], slot=("st", 0))
                DMA("dma_start", ARGS(out=shard[("kd", l, c)].rearrange("(g r) n -> r g n", g=2), in_=DKT[:]),
                    r=["DKT"], w=[("kvs", "kd", l, c)], slot=("st", 1))
                DMA("dma_start", ARGS(out=shard[("vm", l, c)].rearrange("(h p) n -> p h n", h=8),
                                      in_=Vst[:].rearrange("p h t d -> p h (t d)")),
                    r=["Vst"], w=[("kvs", "vm", l, c)], slot=("st", 2))
                DMA("dma_start", ARGS(out=shard[("vd", l, c)].rearrange("(h p) n -> p h n", h=4),
                                      in_=VDst[:].rearrange("p h t d -> p h (t d)")),
                    r=["VDst"], w=[("kvs", "vd", l, c)], slot=("st", 3))
                if fused and cc_now:
                    for nm in ("km", "kd", "vm", "vd"):
                        R.add("pool", "collective_compute",
                              ARGS("AllGather", ALU.bypass, replica_groups=GROUPS,
                                   ins=[shard[(nm, l, c)].opt()], outs=[gath[(nm, l, c)].opt()]),
                              reads=[("kvs", nm, l, c)], writes=[("kvg", nm, l, c)],
                              is_dma=True, slot=("cc", nm), cc=True)

        def M_tile(l, mt, cx):
            u = cx.u
            b0, b1 = cx.b
            DMA("dma_start", ARGS(out=cx.xt[:], in_=mem_d[mt * 128:(mt + 1) * 128, :]), w=[("xt", u)], slot=("xt", u))
            yield
            yield from norm_transpose(cx, G["mem_norm_g"][:], "g_mem_norm_g", cx.xt[:], ("xt", u), T8, ("T8", mt), mt * 128)
            msl = slice(mt * 128, (mt + 1) * 128)
            yield from pad(PADP)
            for kc in range(8):
                T("matmul", ARGS(ps[b1][:], lhsT=T8[:, kc, msl], rhs=Wmem[:, kc, :],
                                 start=(kc == 0), stop=(kc == 7)), r=[("T8", mt), "WKO"], w=["ps%d" % b1])
            yield
            k3 = ps[b1][:, 0:256].rearrange("p (h d) -> p h d", h=4)
            yield from head_norm(cx, k3, "ps%d" % b1, 4, 64, "mem_kn_g", 0, cx.f3[1][:, 0:4, 0:64], ("f3", u, 1))
            V("tensor_copy", ARGS(out=cx.Kb[:, 0:4, 0:64], in_=cx.f3[1][:, 0:4, 0:64]), r=[("f3", u, 1)], w=[("Kb", u)])
            yield
            yield from pad(PADP)
            for h in range(4):
                T("transpose", ARGS(out=psb(b0)[0:64, h * 128:(h + 1) * 128], in_=cx.Kb[:, h, 0:64],
                                    identity=identb[:]), r=[("Kb", u), "identb"], w=["ps%d" % b0])
            yield
            V("tensor_copy", ARGS(out=memKT[:, :, msl], in_=psb(b0)[0:64, 0:512].rearrange("p (h n) -> p h n", h=4)),
              r=["ps%d" % b0], w=["memKT"])
            yield
            V("tensor_copy", ARGS(out=memV[:, mt, :, 0:64], in_=ps[b1][:, 256:512].rearrange("p (h d) -> p h d", h=4)),
              r=["ps%d" % b1], w=["memV"])
            yield

        def mem_kv(l):
            ensure_wko("M", l)
            interleave([M_tile(l, 0, cxs[0]), M_tile(l, 1, cxs[1])])

        def Q_tile(l, c, t, cx):
            u = cx.u
            b0, b1 = cx.b
            B1 = "ps%d" % b1
            B0 = "ps%d" % b0
            ct = c * 4 + t
            par = c % 2
            Y, QT, MQT, DQc = Ys[par], QTs[par], MQTs[par], DQcs[par]
            tsl = slice(t * 128, (t + 1) * 128)
            DMA("dma_start", ARGS(out=cx.xt[:], in_=x_tile_src(l, c, t)), r=[("xd", l, c, t)], w=[("xt", u)], slot=("xt", u))
            for _ in range(PADX):
                yield
            yield from norm_transpose(cx, Gn[l][:], "g_norm%d" % l, cx.xt[:], ("xt", u), T8, ("T8", t), t * 128)
            for kc in range(8):
                T("matmul", ARGS(ps[b1][:], lhsT=T8[:, kc, tsl], rhs=WinQ[:, kc, 0:512],
                                 start=(kc == 0), stop=(kc == 7)), r=[("T8", t), "WinQ0", "WinQ1"], w=[B1])
            yield
            yield from pad(PADA)
            A("activation", ARGS(out=cx.sq[:, 0:256], in_=ps[b1][:, 0:256], func=AF.Square,
                                 accum_out=cx.ss[2][:, 0:1]), r=[B1], w=[("sq", u), ("ss", u, 2)])
            yield
            rr = yield from rstd_of(cx, cx.ss[2][:, 0:1], 1, 256.0, 2, ("ss", u, 2))
            V("scalar_tensor_tensor", ARGS(out=cx.c256[:], in0=ps[b1][:, 0:256], scalar=rr,
                                           in1=G["mla_q_norm_g"][:], op0=ALU.mult, op1=ALU.mult),
              r=[B1, ("rs", u, 2), "g_mla_q_norm_g"], w=[("c256", u)])
            yield
            yield from pad(PADP)
            for j in range(2):
                T("transpose", ARGS(out=psb(b0)[:, j * 128:(j + 1) * 128], in_=cx.c256[:, j * 128:(j + 1) * 128],
                                    identity=identb[:]), r=[("c256", u), "identb"], w=[B0])
            yield
            V("tensor_copy", ARGS(out=cx.cT[:], in_=psb(b0)[:, 0:256].rearrange("p (j n) -> p j n", j=2)),
              r=[B0], w=[("cT", u)])
            yield
            dq3 = ps[b1][:, 256:512].rearrange("p (m d) -> p m d", m=8)
            yield from head_norm(cx, dq3, B1, 8, 32, "diff_qn_g", 0, cx.f3[1][:, :, 0:32], ("f3", u, 1),
                                 scale=1.0 / math.sqrt(32.0))
            yield from diff_pack(cx, ct)
            V("tensor_copy", ARGS(out=DQc[:, :, tsl], in_=psb(b0)[:, 0:256].rearrange("p (g n) -> p g n", g=2)),
              r=[B0], w=[("DQc", par)])
            yield
            yield from pad(PADP)
            for kc in range(8):
                T("matmul", ARGS(ps[b1][:, 0:256], lhsT=T8[:, kc, tsl], rhs=WinQ[:, kc, 512:768],
                                 start=(kc == 0), stop=(kc == 7)), r=[("T8", t), "WinQ2"], w=[B1])
            yield
            mq3 = ps[b1][:, 0:256].rearrange("p (h d) -> p h d", h=4)
            yield from head_norm(cx, mq3, B1, 4, 64, "mem_qn_g", 3, cx.f3[1][:, 0:4, 0:64], ("f3", u, 1), scale=0.125)
            V("tensor_copy", ARGS(out=cx.Kb[:, 0:4, 0:64], in_=cx.f3[1][:, 0:4, 0:64]), r=[("f3", u, 1)], w=[("Kb", u)])
            yield
            yield from pad(PADP)
            for h in range(4):
                T("transpose", ARGS(out=psb(b0)[0:64, h * 128:(h + 1) * 128], in_=cx.Kb[:, h, 0:64],
                                    identity=identb[:]), r=[("Kb", u), "identb"], w=[B0])
            yield
            V("tensor_copy", ARGS(out=MQT[:, :, tsl], in_=psb(b0)[0:64, 0:512].rearrange("p (h n) -> p h n", h=4)),
              r=[B0], w=[("MQT", par)])
            yield
            for n in range(2):
                yield from pad(PADP)
                for kc in range(8):
                    T("matmul", ARGS(ps[b1][:], lhsT=T8[:, kc, tsl], rhs=WinQ[:, kc, 768 + n * 512:768 + (n + 1) * 512],
                                     start=(kc == 0), stop=(kc == 7)), r=[("T8", t), "WinQ3"], w=[B1])
                yield
                th = cx.sq[:, n * 512:(n + 1) * 512]
                yield from pad(PADA)
                A("activation", ARGS(out=th, in_=ps[b1][:], func=AF.Tanh, scale=0.5), r=[B1], w=[("sq", u)])
                yield
                V("tensor_scalar", ARGS(out=th, in0=th, scalar1=0.5, scalar2=0.5, op0=ALU.mult, op1=ALU.add),
                  r=[("sq", u)], w=[("sq", u)])
                yield
                V("tensor_tensor", ARGS(out=Y[:, t, n * 512:(n + 1) * 512], in0=ps[b1][:], in1=th, op=ALU.mult),
                  r=[B1, ("sq", u)], w=[("Y", par, t)])
                yield
            for n in range(2):
                yield from pad(PADP)
                for j in range(2):
                    T("matmul", ARGS(ps[b1][:, 0:384], lhsT=cx.cT[:, j, :], rhs=Wuq[:, j, n * 384:(n + 1) * 384],
                                     start=(j == 0), stop=(j == 1)), r=[("cT", u), "Wuq"], w=[B1])
                yield
                q3 = ps[b1][:, 0:384].rearrange("p (h d) -> p h d", h=4)
                yield from pad(PADA)
                A("activation", ARGS(out=cx.sq[:, n * 384:(n + 1) * 384].rearrange("p (h d) -> p h d", h=4),
                                     in_=q3, func=AF.Square), r=[B1], w=[("sq", u)])
                yield
                V("tensor_tensor", ARGS(out=cx.f3[0][:, n * 4:(n + 1) * 4, :], in0=q3,
                                        in1=G["mla_qn_g"][:].unsqueeze(1).to_broadcast([128, 4, 96]),
                                        op=ALU.mult), r=[B1, "g_mla_qn_g"], w=[("f3", u, 0)])
                yield
            V("tensor_reduce", ARGS(out=cx.ss[1][:, 0:8], in_=cx.sq[:, 0:768].rearrange("p (h d) -> p h d", h=8),
                                    axis=AX.X, op=ALU.add), r=[("sq", u)], w=[("ss", u, 1)])
            yield
            rq = yield from rstd_of(cx, cx.ss[1][:, 0:8], 8, 96.0, 1, ("ss", u, 1))
            V("tensor_scalar", ARGS(out=rq, in0=rq, scalar1=1.0 / math.sqrt(96.0), scalar2=None, op0=ALU.mult),
              r=[("rs", u, 1)], w=[("rs", u, 1)])
            yield
            V("tensor_tensor", ARGS(out=cx.f3[0][:], in0=rq.unsqueeze(2).to_broadcast([128, 8, 96]),
                                    in1=cx.f3[0][:], op=ALU.mult), r=[("rs", u, 1), ("f3", u, 0)], w=[("f3", u, 0)])
            yield
            V("tensor_copy", ARGS(out=cx.Kb[:, :, 0:64], in_=cx.f3[0][:, :, 0:64]), r=[("f3", u, 0)], w=[("Kb", u)])
            yield
            yield from rope(cx, cx.f3[0], ("f3", u, 0), cx.Kb, ("Kb", u), 64, ct, 8)
            yield from pad(PADP)
            for h in range(8):
                T("transpose", ARGS(out=psb(b0)[0:96, h * 128:(h + 1) * 128], in_=cx.Kb[:, h, :],
                                    identity=identb[:]), r=[("Kb", u), "identb"], w=[B0])
                if h % 4 == 3:
                    yield
            for hf in range(2):
                V("tensor_copy", ARGS(out=QT[:, 4 * hf:4 * hf + 4, tsl],
                                      in_=psb(b0)[0:96, 512 * hf:512 * hf + 512].rearrange("p (h n) -> p h n", h=4)),
                  r=[B0], w=[("QT", par)])
                yield

        def O_tile(l, c, t, cx):
            u = cx.u
            b0, b1 = cx.b
            B1 = "ps%d" % b1
            B0 = "ps%d" % b0
            par = c % 2
            Y = Ys[par]
            tsl = slice(t * 128, (t + 1) * 128)
            DMA("dma_start", ARGS(out=cx.xt[:], in_=x_tile_src(l, c, t)), r=[("xd", l, c, t)], w=[("xt", u)], slot=("xt", u))
            yield
            yield from pad(PADP)
            for kc in range(8):
                T("transpose", ARGS(out=psb(b0)[:, kc * 128:(kc + 1) * 128],
                                    in_=Y[:, t, kc * 128:(kc + 1) * 128], identity=identb[:]),
                  r=[("Y", par, t), "identb"], w=[B0])
                if kc % 4 == 3:
                    yield
            for hf in range(2):
                V("tensor_copy", ARGS(out=T8[:, 4 * hf:4 * hf + 4, tsl],
                                      in_=psb(b0)[:, 512 * hf:512 * hf + 512].rearrange("p (k n) -> p k n", k=4)),
                  r=[B0], w=[("T8", t)])
                yield
            for n in range(2):
                yield from pad(PADP)
                for kc in range(8):
                    T("matmul", ARGS(ps[b1][:], lhsT=T8[:, kc, tsl], rhs=Wout[:, kc, n * 512:(n + 1) * 512],
                                     start=(kc == 0), stop=(kc == 7)), r=[("T8", t), "WKO"], w=[B1])
                yield
                V("tensor_tensor", ARGS(out=cx.xt[:, n * 512:(n + 1) * 512], in0=ps[b1][:],
                                        in1=cx.xt[:, n * 512:(n + 1) * 512], op=ALU.add),
                  r=[B1, ("xt", u)], w=[("xt", u)])
                yield
            DMA("dma_start", ARGS(out=xd[l + 1][c].rearrange("(p t) f -> p t f", t=4)[:, t, :], in_=cx.xt[:]),
                r=[("xt", u)], w=[("xd", l + 1, c, t)], slot=("xst", u))
            yield

        kvctr = [0, 0]
        dqctr = [0]
        BG_STEPS = 4
        PADA = 6
        PADP = 3
        PADC = 45
        PADX = 16
        PADW = 80

        def attention(l, c, bg=None):
            par = c % 2
            Y, QT, MQT, DQc = Ys[par], QTs[par], MQTs[par], DQcs[par]
            YK = [("Y", par, t) for t in range(4)]
            kmg = [gath[("km", l, b)].rearrange("(j h d) n -> d j h n", j=4, h=8) for b in range(4)]
            kdg = [gath[("kd", l, b)].rearrange("(j g r) n -> r j g n", j=4, g=2) for b in range(4)]
            vmg = [gath[("vm", l, b)].rearrange("(j h p) n -> p j h n", j=4, h=8) for b in range(4)]
            vdg = [gath[("vd", l, b)].rearrange("(j h p) n -> p j h n", j=4, h=4) for b in range(4)]
            if True:
                units = []
                for h in range(4):
                    units.append(("mem", h))
                for h in range(8):
                    units.append(("mla", h))
                for m in range(8):
                    units.append(("dif", m))
                bidc = [0]
                tiles = []
                for ui, (kind, h) in enumerate(units):
                    if kind == "mem":
                        tl = []
                        for mt in range(2):
                            tl.append(dict(kap=memKT[:, h, mt * 128:(mt + 1) * 128], vap=memV[:, mt, h, :],
                                           mask=None, kr="memKT", vr="memV", load=None))
                        qap, qk_, scale = MQT[:, h, :], ("MQT", par), 1.0
                        padfn = None
                    else:
                        d = 96 if kind == "mla" else 128
                        kg = kmg if kind == "mla" else kdg
                        vg = vmg if kind == "mla" else vdg
                        vh = h if kind == "mla" else h // 2
                        padfn = None
                        if kind == "mla":
                            qap, qk_ = QT[:, h, :], ("QT", par)
                        else:
                            pk = dqctr[0] % 2
                            dqctr[0] += 1
                            qap, qk_ = DQpad[pk][:], ("DQpad", pk)

                            def padfn(pk=pk, h=h):
                                V("tensor_scalar", ARGS(out=DQpad[pk][:], in0=DQc[:, h // 4, :],
                                                        scalar1=rowmask[:, h % 4:h % 4 + 1], scalar2=None, op0=ALU.mult),
                                  r=[("DQc", par), "rowmask"], w=[("DQpad", pk)])
                        scale = 1.0
                        tl = []
                        kn = "km" if kind == "mla" else "kd"
                        vn = "vm" if kind == "mla" else "vd"
                        hk = h if kind == "mla" else h // 4
                        for b in range(c + 1):
                            if b == c:
                                so = kvctr[1] % 2
                                kvctr[1] += 1
                                if kind == "mla":
                                    ksrc = shard[("km", l, c)].rearrange("(h d) n -> d h n", h=8)[:, hk, :]
                                else:
                                    ksrc = shard[("kd", l, c)].rearrange("(g r) n -> r g n", g=2)[:, hk, :]
                                vsrc = shard[(vn, l, c)].rearrange("(h p) n -> p h n", p=128)[:, vh, :]

                                def load_own(so=so, d=d, ksrc=ksrc, vsrc=vsrc, kn=kn, vn=vn):
                                    DMA("dma_start", ARGS(out=Kown[so][0:d, :], in_=ksrc),
                                        r=[("kvs", kn, l, c)], w=[("Kown", so)], slot=("Kown", so))
                                    DMA("dma_start", ARGS(out=Vown[so][:], in_=vsrc),
                                        r=[("kvs", vn, l, c)], w=[("Vown", so)], slot=("Vown", so))
                                bidc[0] += 1
                                for tk in range(4):
                                    tl.append(dict(kap=Kown[so][0:d, tk * 128:(tk + 1) * 128],
                                                   vap=Vown[so][:, tk * 65:(tk + 1) * 65], mask=tk,
                                                   kr=("Kown", so), vr=("Vown", so), bid=bidc[0], slotkey=("own", so),
                                                   load=load_own if tk == 0 else None))
                            s = kvctr[0] % NKV
                            kvctr[0] += 1

                            def load(s=s, b=b, d=d, kg=kg, vg=vg, vh=vh, hk=hk, kn=kn, vn=vn, diag=(b == c)):
                                DMA("dma_start", ARGS(out=Kbuf[s][0:d, :].rearrange("d (j n) -> d j n", j=4),
                                                      in_=kg[b][:, :, hk, :]),
                                    r=[("kvg", kn, l, b)], w=[("Kbuf", s)], slot=("Kbuf", s))
                                DMA("dma_start", ARGS(out=Vbuf[s][:], in_=vg[b][:, :, vh, :]),
                                    r=[("kvg", vn, l, b)], w=[("Vbuf", s)], slot=("Vbuf", s))

                            def flagfix(s=s):
                                V("tensor_tensor", ARGS(out=Vbuf[s][:],
                                                        in0=flg[:, c * 4:(c + 1) * 4].unsqueeze(2).to_broadcast([128, 4, 260]),
                                                        in1=Vbuf[s][:], op=ALU.mult),
                                  r=["flg", ("Vbuf", s)], w=[("Vbuf", s)])
                            bidc[0] += 1
                            for j in range(4):
                                for tk in range(4):
                                    tl.append(dict(kap=Kbuf[s][0:d, j * 512 + tk * 128: j * 512 + (tk + 1) * 128],
                                                   vap=Vbuf[s][:, j, tk * 65:(tk + 1) * 65], mask=None,
                                                   kr=("Kbuf", s), vr=("Vbuf", s), bid=bidc[0], slotkey=("kv", s),
                                                   fix=flagfix if (b == c and j == 0 and tk == 0) else None,
                                                   load=load if (j == 0 and tk == 0) else None))
                    for i, td in enumerate(tl):
                        td.update(ui=ui, first=(i == 0), last=(i == len(tl) - 1), qap=qap, qk=qk_, scale=scale,
                                  pad=padfn if i == 0 else None)
                        tiles.append(td)

                def post_unit(ui):
                    kind, h = units[ui]
                    acc = 4
                    o = 0
                    V("tensor_copy", ARGS(out=OT[o][:], in_=ps[acc][0:65, :]), r=["ps%d" % acc], w=[("OT", o)])

                    def pe_part():
                        for t in range(4):
                            T("transpose", ARGS(out=ps[5][:, t * 65:(t + 1) * 65], in_=OT[o][0:65, t * 128:(t + 1) * 128],
                                                         identity=identf[0:65, 0:65]), r=[("OT", o), "identf"], w=["ps5"])
                        o3 = ps[5][:, 0:260].rearrange("p (t d) -> p t d", t=4)
                        ri = ui % 2
                        V("reciprocal", ARGS(out=rec[ri][:], in_=o3[:, :, 64]), r=["ps5"], w=[("rec", ri)])
                        if kind in ("mla", "mem"):
                            col = h * 64 if kind == "mla" else 768 + h * 64
                            V("tensor_tensor", ARGS(out=Dh[0][:], in0=rec[ri][:].unsqueeze(2).to_broadcast([128, 4, 64]),
                                                        in1=o3[:, :, 0:64], op=ALU.mult), r=[("rec", ri), "ps5"], w=["Dh0"])
                            V("tensor_tensor", ARGS(out=Y[:, :, col:col + 64], in0=Dh[0][:], in1=Y[:, :, col:col + 64],
                                                        op=ALU.mult), r=["Dh0"] + YK, w=YK)
                        else:
                            hh, mj = h // 2, h % 2
                            col = 512 + hh * 64
                            if mj == 0:
                                V("tensor_tensor", ARGS(out=Dh[1][:], in0=rec[ri][:].unsqueeze(2).to_broadcast([128, 4, 64]),
                                                            in1=o3[:, :, 0:64], op=ALU.mult), r=[("rec", ri), "ps5"], w=["Dh1"])
                            else:
                                V("tensor_scalar", ARGS(out=rec[ri][:], in0=rec[ri][:], scalar1=nlam[:, 0:1], scalar2=None,
                                                            op0=ALU.mult), r=[("rec", ri), "nlam"], w=[("rec", ri)])
                                V("tensor_tensor", ARGS(out=Dh[0][:], in0=rec[ri][:].unsqueeze(2).to_broadcast([128, 4, 64]),
                                                            in1=o3[:, :, 0:64], op=ALU.mult), r=[("rec", ri), "ps5"], w=["Dh0"])
                                V("tensor_tensor", ARGS(out=Dh[1][:], in0=Dh[1][:], in1=Dh[0][:], op=ALU.add),
                                  r=["Dh0", "Dh1"], w=["Dh1"])
                                V("tensor_tensor", ARGS(out=Dh[0][:], in0=Dh[1][:], in1=Dh[1][:], op=ALU.mult),
                                  r=["Dh1"], w=["Dh0"])
                                V("tensor_reduce", ARGS(out=cxp.ss[0][:, 0:4], in_=Dh[0][:], axis=AX.X, op=ALU.add),
                                  r=["Dh0"], w=[("ss", "p", 0)])
                                run(rstd_of(cxp, cxp.ss[0][:, 0:4], 4, 64.0, 0, ("ss", "p", 0)))
                                rr = cxp.rs[0][:, 0:4]
                                V("tensor_tensor", ARGS(out=Dh[1][:], in0=rr.unsqueeze(2).to_broadcast([128, 4, 64]),
                                                                   in1=Dh[1][:], op=ALU.mult), r=[("rs", "p", 0), "Dh1"], w=["Dh1"])
                                V("tensor_tensor", ARGS(out=Dh[1][:], in0=Dh[1][:],
                                                            in1=sgs[:].unsqueeze(1).to_broadcast([128, 4, 64]), op=ALU.mult),
                                  r=["Dh1", "sgs"], w=["Dh1"])
                                V("tensor_tensor", ARGS(out=Y[:, :, col:col + 64], in0=Dh[1][:], in1=Y[:, :, col:col + 64],
                                                            op=ALU.mult), r=["Dh1"] + YK, w=YK)
                    return pe_part

                nt = len(tiles)
                pending = []
                assert nt % 2 == 0
                npair = nt // 2
                blk_last = {}
                for i, td in enumerate(tiles):
                    if td.get("bid") is not None:
                        blk_last[td["bid"]] = i
                sched = {}
                prev_on_slot = {}
                for i, td in enumerate(tiles):
                    if td["load"] is not None:
                        pb = prev_on_slot.get(td["slotkey"])
                        rp = -1 if pb is None else blk_last[pb] // 2
                        assert rp < i // 2 - 2
                        sched.setdefault(rp, []).append(i)
                        prev_on_slot[td["slotkey"]] = td["bid"]
                issued = set()

                def issue(i0):
                    if i0 not in issued:
                        issued.add(i0)
                        tiles[i0]["load"]()

                def do_qk(i):
                    td = tiles[i]
                    if td["load"] is not None:
                        issue(i)
                    if td.get("fix") is not None:
                        td["fix"]()
                    if td.get("pad") is not None:
                        td["pad"]()
                    pr = (i // 2) % 2
                    bk = 2 * pr + (i % 2)
                    T("matmul", ARGS(ps[bk][:], lhsT=td["kap"], rhs=td["qap"], start=True, stop=True),
                      r=[td["kr"], td["qk"]], w=["ps%d" % bk])

                def do_exp(p):
                    pr = p % 2
                    pt = p % NPT
                    A("activation", ARGS(out=PT[pt][:], in_=psall[:, 2 * pr:2 * pr + 2, :], func=AF.Exp),
                      r=["ps%d" % (2 * pr), "ps%d" % (2 * pr + 1)], w=[("PT", pt)])
                    t0, t1 = tiles[2 * p], tiles[2 * p + 1]
                    if t0["mask"] is not None:
                        assert t1["mask"] == t0["mask"] + 1
                        P("affine_select", ARGS(out=PT[pt][:], in_=PT[pt][:], pattern=[[-1, 2], [1, 4], [4, 128]],
                                                compare_op=ALU.is_ge, fill=0.0, base=-t0["mask"], channel_multiplier=-4),
                          r=[("PT", pt)], w=[("PT", pt)])

                def do_pv(p):
                    pt = p % NPT
                    for k in range(2):
                        td = tiles[2 * p + k]
                        acc = 4
                        T("matmul", ARGS(ps[acc][0:65, :], lhsT=td["vap"], rhs=PT[pt][:, k, :], start=td["first"], stop=td["last"]),
                          r=[td["vr"], ("PT", pt)], w=["ps%d" % acc])
                        if td["last"]:
                            fn = post_unit(td["ui"])
                            if units[td["ui"]][0] == "mem":
                                fn()
                            else:
                                pending.append([2, fn])

                for i0 in sched.get(-1, []):
                    issue(i0)
                for i in range(min(4, nt)):
                    do_qk(i)
                for p in range(npair):
                    do_exp(p)
                    if p + 2 < npair:
                        do_qk(2 * p + 4)
                        do_qk(2 * p + 5)
                    do_pv(p)
                    for i0 in sched.get(p, []):
                        issue(i0)
                    if bg is not None:
                        for _ in range(BG_STEPS):
                            next(bg, None)
                    for pnd in list(pending):
                        pnd[0] -= 1
                        if pnd[0] <= 0:
                            pnd[1]()
                            pending.remove(pnd)
                for pnd in pending:
                    pnd[1]()
                pending.clear()

            if bg is not None:
                run(bg)

        assert fused, "only the fused single-launch program is built"

        def pair2(gen_fn, l, c):
            interleave([gen_fn(l, c, 0, cxs[0]), gen_fn(l, c, 1, cxs[1])])
            interleave([gen_fn(l, c, 2, cxs[0]), gen_fn(l, c, 3, cxs[1])])

        def A_bg(l, c):
            if wko[0] != ("K", l):
                ensure_wko("K", l)
                for _ in range(PADW):
                    yield
            for t in range(4):
                yield from A_tile(l, c, t, cxs[1])
            kv_store(l, c)
            yield

        def cc_bg(l, cs):
            for c in cs:
                for nm in ("km", "kd", "vm", "vd"):
                    kv_cc(l, c, nm)
                    for _ in range(PADC):
                        yield

        def O_bg(l, c):
            if wko[0] != ("O", l):
                ensure_wko("O", l)
                for _ in range(PADW):
                    yield
            for t in range(4):
                yield from O_tile(l, c, t, cxs[1])

        def Q_bg(l, c, reload=False):
            if reload:
                load_Q(l)
                for _ in range(PADW):
                    yield
            for t in range(4):
                yield from Q_tile(l, c, t, cxs[1])

        def M_bg(l):
            load_M(l)
            ensure_wko("M", l)
            for _ in range(PADW):
                yield
            for mt in range(2):
                yield from M_tile(l, mt, cxs[1])

        def chain(gens):
            for g in gens:
                yield from g

        for l_ in range(2):
            DMA("dma_start", ARGS(out=Gn[l_][:], in_=W["norm_g"][l_].partition_broadcast(128)),
                w=["g_norm%d" % l_], slot="g_norm%d" % l_)
        load_Kg(0)
        ensure_wko("K", 0)
        for c in range(4):
            pair2(A_tile, 0, c)
            kv_store(0, c)
        for l in range(2):
            if l == 0 or not BG_ON:
                load_M(l)
                mem_kv(l)
            load_P(l)
            if l == 0:
                load_Kg(1)
                load_Q(0)
                pair2(Q_tile, 0, 0)
            for c in range(4):
                parts = []
                if c >= 1:
                    parts.append(O_bg(l, c - 1))
                if l == 0 and c == 2:
                    parts.append(A_bg(1, 0))
                if l == 0 and c == 3:
                    parts.append(A_bg(1, 1))
                    parts.append(A_bg(1, 2))
                if l == 1 and c == 2:
                    parts.append(A_bg(1, 3))
                if c <= 2:
                    parts.append(Q_bg(l, c + 1))
                elif l == 0:
                    parts.append(Q_bg(1, 0, reload=True))
                    parts.append(M_bg(1))
                attention(l, c, chain(parts) if BG_ON else None)
                if not BG_ON:
                    run(chain(parts))
            ensure_wko("O", l)
            pair2(O_tile, l, 3)
        finals = [("dma", ("xst", 0)), ("dma", ("xst", 1))] + [("dma", ("st", k)) for k in range(4)]
        print("[kernel] sbuf bytes remaining:", nc.sbuf_bytes_remaining)
        R.emit(final_waits=finals)
        build.last_counts = dict(R.counts)
    return nc


_NC_CACHE = {}


def _get_nc(mode):
    if mode not in _NC_CACHE:
        _NC_CACHE[mode] = build(mode)
    return _NC_CACHE[mode]


def _flag_table(j):
    t = np.zeros(16, np.float32)
    for lc in range(4):
        for jj in range(4):
            t[lc * 4 + jj] = 1.0 if gchunk(lc, jj) < gchunk(lc, j) else 0.0
    return t


def _common_inputs(inputs, b, j):
    d = {}
    for k in ["norm_g", "w_in", "mla_q_norm_g", "mla_kv_norm_g", "w_uq", "w_ukv", "mla_qn_g", "mla_kn_g",
              "diff_qn_g", "diff_kn_g", "diff_subln_g", "mem_norm_g", "w_mem_kv", "mem_qn_g", "mem_kn_g", "w_out"]:
        d[k] = np.ascontiguousarray(inputs[k], dtype=np.float32)
    d["diff_lambda"] = np.ascontiguousarray(inputs["diff_lambda"], dtype=np.float32).reshape(2, 128)
    pos = np.asarray(inputs["positions"])[b].astype(np.int32)
    d["pos"] = np.stack([pos[gchunk(lc, j) * 512:(gchunk(lc, j) + 1) * 512] for lc in range(4)])
    d["flg"] = _flag_table(j)
    d["mem"] = np.ascontiguousarray(inputs["mem"][b], dtype=np.float32)
    return d


def _chunks(arr_b, j):
    return np.stack([arr_b[gchunk(lc, j) * 512:(gchunk(lc, j) + 1) * 512] for lc in range(4)])


FUSED = True
BG_ON = True


def kernel(**inputs):
    x = np.asarray(inputs["x"], dtype=np.float32)
    cores = [(b, j) for b in range(2) for j in range(4)]
    ids = list(range(8))
    common = [_common_inputs(inputs, b, j) for (b, j) in cores]
    xc = [_chunks(x[b], j) for (b, j) in cores]

    def scatter(res):
        out = np.empty_like(x)
        for ci, (b, j) in enumerate(cores):
            o = np.asarray(res[ci]["out"], dtype=np.float32)
            for lc in range(4):
                g = gchunk(lc, j)
                out[b, g * 512:(g + 1) * 512] = o[lc]
        return out

    if FUSED:
        inf = [dict(common[i], xc=xc[i]) for i in ids]
        rf = run_bass_kernel_spmd(_get_nc("F"), inf, core_ids=ids).results
        return scatter(rf)

    def gather(res, l):
        g = []
        for ci, (b, j) in enumerate(cores):
            d = {}
            for nm in ("km", "kd", "vm", "vd"):
                for c in range(4):
                    d["%sg%d_%d" % (nm, l, c)] = np.concatenate(
                        [res[b * 4 + jj]["%s%d_%d" % (nm, l, c)] for jj in range(4)], axis=0)
            g.append(d)
        return g

    in1 = [dict(common[i], xc=xc[i]) for i in ids]
    r1 = run_bass_kernel_spmd(_get_nc("A0"), in1, core_ids=ids).results
    g0 = gather(r1, 0)
    in2 = [dict(common[i], xc=xc[i], **g0[i]) for i in ids]
    r2 = run_bass_kernel_spmd(_get_nc("B0A1"), in2, core_ids=ids).results
    g1 = gather(r2, 1)
    in3 = [dict(common[i], x1=r2[i]["x1"], **g1[i]) for i in ids]
    r3 = run_bass_kernel_spmd(_get_nc("B1"), in3, core_ids=ids).results
    return scatter(r3)
```
